# Optimizing a Trainium2 kernel written in Bass

```python
import math
import jax, jax.numpy as jnp
from jax import lax
import numpy as np

D_MODEL = 1024
BATCH = 8
SEQ = 4096
DEPTH = 2

CONV_CH = D_MODEL // 2
CONV_GROUPS = 8
CONV_WIDTH = 31
MLA_HEADS = 8
QK_NOPE = 64
QK_ROPE = 32
V_HEAD = 64
Q_LORA = 384
KV_LORA = 256
ROPE_THETA = 10000.0
Q_BLOCK = 128
MIX_WIDTH = CONV_CH + MLA_HEADS * V_HEAD
IN_COLS = 2 * CONV_CH + Q_LORA + KV_LORA + QK_ROPE
POOL_WINDOWS = (2, 4, 8, 16)
POOL_GROUPS = len(POOL_WINDOWS)
POOL_GROUP = D_MODEL // POOL_GROUPS
D_FF = 2816
FFN_CONV = 3
EPS = 1e-6
N_EVEN = (DEPTH + 1) // 2
N_ODD = DEPTH // 2

kernel_name = "hybrid_conv_mla_pool_convffn"


def rmsnorm(x, g):
    xf = x.astype(jnp.float32)
    y = xf * lax.rsqrt(jnp.mean(xf * xf, axis=-1, keepdims=True) + EPS)
    return (y * g.astype(jnp.float32)).astype(x.dtype)


def layernorm(x, g, b):
    xf = x.astype(jnp.float32)
    mu = jnp.mean(xf, axis=-1, keepdims=True)
    var = jnp.mean(jnp.square(xf - mu), axis=-1, keepdims=True)
    y = (xf - mu) * lax.rsqrt(var + EPS)
    return (y * g.astype(jnp.float32) + b.astype(jnp.float32)).astype(x.dtype)


def causal_dwconv(x, w, b):
    K, C = w.shape
    y = lax.conv_general_dilated(
        x, w[:, None, :].astype(x.dtype), window_strides=(1,), padding=[(K - 1, 0)],
        dimension_numbers=("NWC", "WIO", "NWC"), feature_group_count=C)
    return y + b.astype(x.dtype)


def rope_tables(positions):
    inv_freq = 1.0 / (ROPE_THETA ** (jnp.arange(0, QK_ROPE, 2, dtype=jnp.float32) / QK_ROPE))
    ang = positions.astype(jnp.float32)[..., None] * inv_freq
    return jnp.cos(ang), jnp.sin(ang)


def apply_rope(x, cos, sin):
    xf = x.astype(jnp.float32)
    x1, x2 = jnp.split(xf, 2, axis=-1)
    return jnp.concatenate([x1 * cos - x2 * sin, x1 * sin + x2 * cos], axis=-1).astype(x.dtype)


def mla_attention(qn, qr, kn, kr, v):
    S = qn.shape[1]
    scale = 1.0 / math.sqrt(QK_NOPE + QK_ROPE)
    outs = []
    for i in range(S // Q_BLOCK):
        q0, k_end = i * Q_BLOCK, (i + 1) * Q_BLOCK
        s = (jnp.einsum("bqhd,bkhd->bhqk", qn[:, q0:k_end], kn[:, :k_end])
             + jnp.einsum("bqhr,bkr->bhqk", qr[:, q0:k_end], kr[:, :k_end]))
        s = s.astype(jnp.float32) * scale
        mask = (q0 + jnp.arange(Q_BLOCK))[:, None] >= jnp.arange(k_end)[None, :]
        s = jnp.where(mask, s, jnp.finfo(jnp.float32).min)
        p = jax.nn.softmax(s, axis=-1).astype(v.dtype)
        outs.append(jnp.einsum("bhqk,bkhd->bqhd", p, v[:, :k_end]))
    o = jnp.concatenate(outs, axis=1)
    return o.reshape(o.shape[0], S, MLA_HEADS * V_HEAD)


def conv_mla_mixer(h, cos, sin, w_in, conv_w, conv_b, conv_ln_g, conv_ln_b,
                   q_norm_g, w_uq, kv_norm_g, w_ukv, w_out):
    B, S, _ = h.shape
    proj = h @ w_in
    c0 = 2 * CONV_CH
    a, g = proj[..., :CONV_CH], proj[..., CONV_CH:c0]
    u = a * jax.nn.sigmoid(g)
    u = causal_dwconv(u, conv_w, conv_b)
    u = jax.nn.silu(layernorm(u, conv_ln_g, conv_ln_b))
    cq = proj[..., c0:c0 + Q_LORA]
    ckv = proj[..., c0 + Q_LORA:c0 + Q_LORA + KV_LORA]
    kr = proj[..., c0 + Q_LORA + KV_LORA:]
    q = (rmsnorm(cq, q_norm_g) @ w_uq).reshape(B, S, MLA_HEADS, QK_NOPE + QK_ROPE)
    qn, qr = q[..., :QK_NOPE], q[..., QK_NOPE:]
    qr = apply_rope(qr, cos[:, :, None, :], sin[:, :, None, :])
    kr = apply_rope(kr, cos, sin)
    kv = (rmsnorm(ckv, kv_norm_g) @ w_ukv).reshape(B, S, MLA_HEADS, QK_NOPE + V_HEAD)
    kn, v = kv[..., :QK_NOPE], kv[..., QK_NOPE:]
    o = mla_attention(qn, qr, kn, kr, v)
    return jnp.concatenate([u, o], axis=-1) @ w_out


def pool_mixer(h, pool_w, pool_scale):
    B, S, _ = h.shape
    hf = h.astype(jnp.float32).reshape(B, S, POOL_GROUPS, POOL_GROUP)
    t = jnp.arange(S)
    pooled = []
    for gi, w in enumerate(POOL_WINDOWS):
        xg = hf[:, :, gi]
        cs = jnp.cumsum(xg, axis=1)
        lag = jnp.pad(cs, ((0, 0), (w, 0), (0, 0)))[:, :S]
        cnt = jnp.minimum(t + 1, w).astype(jnp.float32)[None, :, None]
        pooled.append((cs - lag) / cnt - xg)
    p = jnp.stack(pooled, axis=2).astype(h.dtype)
    y = jnp.einsum("bsgc,gcd->bsgd", p, pool_w).reshape(B, S, D_MODEL)
    return y * pool_scale


def conv_ffn(h, w_up, ffn_conv_w, ffn_conv_b, w_down):
    uv = h @ w_up
    u, v = uv[..., :D_FF], uv[..., D_FF:]
    u = causal_dwconv(u, ffn_conv_w, ffn_conv_b)
    return (jax.nn.silu(u) * v) @ w_down


def setup_inputs(seed: int = 0) -> dict:
    key = jax.random.key(seed)
    ks = iter(jax.random.split(key, 40))
    f32 = jnp.float32

    def nrm(shape, fan_in):
        return jax.random.normal(next(ks), shape, f32) * (fan_in ** -0.5)

    def gain(shape):
        return 1.0 + 0.02 * jax.random.normal(next(ks), shape, f32)

    def bias(shape):
        return 0.02 * jax.random.normal(next(ks), shape, f32)

    x = jax.random.normal(next(ks), (BATCH, SEQ, D_MODEL), f32)
    offsets = jax.random.randint(next(ks), (BATCH, 1), 0, 1024, dtype=jnp.int32)
    positions = offsets + jnp.arange(SEQ, dtype=jnp.int32)[None, :]
    E, O, L = N_EVEN, N_ODD, DEPTH
    return {
        "x": x,
        "positions": positions,
        "norm_mix_e": gain((E, D_MODEL)),
        "w_in": nrm((E, D_MODEL, IN_COLS), D_MODEL),
        "conv_w": nrm((E, CONV_WIDTH, CONV_CH), CONV_WIDTH),
        "conv_b": bias((E, CONV_CH)),
        "conv_ln_g": gain((E, CONV_CH)),
        "conv_ln_b": bias((E, CONV_CH)),
        "q_norm_g": gain((E, Q_LORA)),
        "w_uq": nrm((E, Q_LORA, MLA_HEADS * (QK_NOPE + QK_ROPE)), Q_LORA),
        "kv_norm_g": gain((E, KV_LORA)),
        "w_ukv": nrm((E, KV_LORA, MLA_HEADS * (QK_NOPE + V_HEAD)), KV_LORA),
        "w_out": nrm((E, MIX_WIDTH, D_MODEL), MIX_WIDTH),
        "norm_mix_o": gain((O, D_MODEL)),
        "pool_w": nrm((O, POOL_GROUPS, POOL_GROUP, POOL_GROUP), POOL_GROUP),
        "pool_scale": gain((O, D_MODEL)),
        "norm_ffn": gain((L, D_MODEL)),
        "w_up": nrm((L, D_MODEL, 2 * D_FF), D_MODEL),
        "ffn_conv_w": nrm((L, FFN_CONV, D_FF), FFN_CONV),
        "ffn_conv_b": bias((L, D_FF)),
        "w_down": nrm((L, D_FF, D_MODEL), D_FF),
        "final_norm": gain((D_MODEL,)),
    }


def reference(x, positions, norm_mix_e, w_in, conv_w, conv_b, conv_ln_g, conv_ln_b,
              q_norm_g, w_uq, kv_norm_g, w_ukv, w_out, norm_mix_o, pool_w, pool_scale,
              norm_ffn, w_up, ffn_conv_w, ffn_conv_b, w_down, final_norm):
    cos, sin = rope_tables(positions)
    for l in range(DEPTH):
        if l % 2 == 0:
            e = l // 2
            h = rmsnorm(x, norm_mix_e[e])
            x = x + conv_mla_mixer(h, cos, sin, w_in[e], conv_w[e], conv_b[e],
                                   conv_ln_g[e], conv_ln_b[e], q_norm_g[e], w_uq[e],
                                   kv_norm_g[e], w_ukv[e], w_out[e])
        else:
            o = l // 2
            h = rmsnorm(x, norm_mix_o[o])
            x = x + pool_mixer(h, pool_w[o], pool_scale[o])
        h = rmsnorm(x, norm_ffn[l])
        x = x + conv_ffn(h, w_up[l], ffn_conv_w[l], ffn_conv_b[l], w_down[l])
    return rmsnorm(x, final_norm)
```

```python
import contextlib
import numpy as np
import concourse.bass as bass
import concourse.mybir as mybir
from concourse.bass_utils import run_bass_kernel_spmd

F32 = mybir.dt.float32
BF16 = mybir.dt.bfloat16
I32 = mybir.dt.int32
AF = mybir.ActivationFunctionType
ALU = mybir.AluOpType

ENGS = ("pe", "act", "dve", "pool", "sp")
N_DMA_SEMS = 6
DMA_INFLIGHT = {"sp": 6, "pool": 4, "act": 2, "pe": 2, "dve": 2}

S = 4096
D = 1024
T = 512
NT = S // T
NSUB = S // 128
DFF = 2816
NF = DFF // 128
EPS = 1e-6
TWO_PI = float(2 * np.pi)
C1 = 6.28125
C2 = float(2 * np.pi - 6.28125)
ATT_SCALE = float(1.0 / np.sqrt(96.0))


class Op:
    __slots__ = ("eng", "fn", "deps", "sig", "needs_sig", "is_dma", "dsem", "dval")

    def __init__(self, eng, fn, is_dma):
        self.eng = eng
        self.fn = fn
        self.deps = ()
        self.sig = 0
        self.needs_sig = False
        self.is_dma = is_dma
        self.dsem = None
        self.dval = 0


class Prog:
    def __init__(self, nc):
        self.nc = nc
        self.ops = {e: [] for e in ENGS}
        self.last_w = {}
        self.readers = {}
        self.dma_count = {e: 0 for e in ENGS}
        self.all_ops = []

    def add(self, eng, fn, reads=(), writes=(), dma=False, deps=()):
        op = Op(eng, fn, dma)
        dset = set(d for d in deps if d is not None)
        lw = self.last_w
        rd = self.readers
        for t in reads:
            w = lw.get(t)
            if w is not None:
                dset.add(w)
        for t in writes:
            w = lw.get(t)
            if w is not None:
                dset.add(w)
            r = rd.get(t)
            if r:
                dset.update(r)
        for t in reads:
            rd.setdefault(t, []).append(op)
        for t in writes:
            lw[t] = op
            rd[t] = []
        dset.discard(op)
        op.deps = dset
        if dma:
            k = self.dma_count[eng]
            self.dma_count[eng] = k + 1
            nfl = DMA_INFLIGHT[eng]
            op.dsem = (eng, k % nfl)
            op.dval = 16 * (k // nfl + 1)
        self.ops[eng].append(op)
        self.all_ops.append(op)
        return op

    def pe(self, fn, reads=(), writes=()):
        return self.add("pe", fn, reads, writes)

    def act(self, fn, reads=(), writes=()):
        return self.add("act", fn, reads, writes)

    def dve(self, fn, reads=(), writes=()):
        return self.add("dve", fn, reads, writes)

    def pool(self, fn, reads=(), writes=()):
        return self.add("pool", fn, reads, writes)

    def dma(self, eng, out, in_, reads=(), writes=()):
        return self.add(eng, lambda e: e.dma_start(out=out, in_=in_), reads, writes, dma=True)

    def barrier(self):
        lasts = []
        for e in ENGS:
            if self.ops[e]:
                lasts.append(self.ops[e][-1])
            n = 0
            for op in reversed(self.ops[e]):
                if op.is_dma:
                    lasts.append(op)
                    n += 1
                    if n >= N_DMA_SEMS:
                        break
        for e in ENGS:
            self.add(e, None, deps=lasts)
        self.last_w = {}
        self.readers = {}

    def finish(self, final_deps):
        self.add("sp", None, deps=final_deps)

    def emit(self):
        nc = self.nc
        for op in self.all_ops:
            for d in op.deps:
                if d.is_dma:
                    continue
                if d.eng == "pe" and op.eng == "pe":
                    continue
                d.needs_sig = True
        for e in ENGS:
            c = 0
            for op in self.ops[e]:
                if op.needs_sig and not op.is_dma:
                    c += 1
                    op.sig = c
        with contextlib.ExitStack() as st:
            esem = {e: st.enter_context(nc.semaphore("s_" + e)) for e in ENGS}
            dsem = {}
            for e in ENGS:
                if self.dma_count[e]:
                    for i in range(N_DMA_SEMS):
                        dsem[(e, i)] = st.enter_context(nc.semaphore("d_%s%d" % (e, i)))
            block = st.enter_context(nc.Block())

            def run(ename, eng):
                seen = {}
                for op in self.ops[ename]:
                    need = {}
                    for d in op.deps:
                        if d.is_dma:
                            key = ("d", d.dsem)
                            v = d.dval
                        else:
                            if d.eng == "pe" and ename == "pe":
                                continue
                            key = ("e", d.eng)
                            v = d.sig
                        if need.get(key, 0) < v:
                            need[key] = v
                    if op.is_dma and op.dval > 16:
                        key = ("d", op.dsem)
                        v = op.dval - 16
                        if need.get(key, 0) < v:
                            need[key] = v
                    for key, v in need.items():
                        if seen.get(key, 0) >= v:
                            continue
                        seen[key] = v
                        sem = dsem[key[1]] if key[0] == "d" else esem[key[1]]
                        eng.wait_ge(sem, v)
                    if op.fn is None:
                        if op.needs_sig:
                            eng.nop().then_inc(esem[ename], 1)
                        continue
                    ins = op.fn(eng)
                    if op.is_dma:
                        ins.then_inc(dsem[op.dsem], 16)
                    elif op.needs_sig:
                        ins.then_inc(esem[ename], 1)

            @block.tensor
            def _(eng):
                run("pe", eng)

            @block.scalar
            def _(eng):
                run("act", eng)

            @block.vector
            def _(eng):
                run("dve", eng)

            @block.gpsimd
            def _(eng):
                run("pool", eng)

            @block.sync
            def _(eng):
                run("sp", eng)


class Ctx:
    SB_BASE = 16384 + 512

    def __init__(self, nc):
        self.nc = nc
        self.P = Prog(nc)
        self.ps = [nc.alloc_psum_tensor("ps%d" % i, [128, 512], F32) for i in range(8)]
        self.off = Ctx.SB_BASE
        self.uid = 0
        self.rot_state = {}
        self.din = {}

    def sb(self, shape, dt, name=None):
        self.uid += 1
        esz = 4 if dt in (F32, I32) else 2
        n = 1
        for s in shape[1:]:
            n *= s
        nbytes = (n * esz + 63) // 64 * 64
        t = self.nc.alloc_sbuf_tensor_at("%s_%d" % (name or "t", self.uid), list(shape), dt, offset=self.off)
        self.last_off = self.off
        self.off += nbytes
        if self.off > 16384 + 212000:
            raise RuntimeError("SBUF overflow: %d" % self.off)
        return t

    def sb_at(self, shape, dt, name, offset):
        self.uid += 1
        return self.nc.alloc_sbuf_tensor_at("%s_%d" % (name, self.uid), list(shape), dt, offset=offset)

    def rot(self, name, lst):
        i = self.rot_state.get(name, 0)
        self.rot_state[name] = i + 1
        return lst[i % len(lst)]

    def inp(self, name, shape, dt=F32):
        shape = list(shape)
        nbytes = 4 * int(np.prod(shape))
        pad = nbytes >= (1 << 20) and name not in ("xin",)
        if name not in self.din:
            dshape = list(shape)
            if pad:
                dshape[-2] += 1
            self.din[name] = (self.nc.dram_tensor(name, dshape, dt, kind="ExternalInput").ap(), pad, shape[-2])
        ap, pad, n = self.din[name]
        if pad:
            ap = ap[:, 0:n, :] if len(shape) == 3 else ap[0:n, :]
        return ap


def mm(P, out, lhsT, rhs, start, stop, reads, writes):
    return P.pe(lambda e: e.matmul(out, lhsT, rhs, start=start, stop=stop), reads, writes)


def setup_common(K):
    P = K.P
    K.ss_all = K.sb([128, NSUB], F32, "ss_all")
    K.rstd_all = K.sb([128, NSUB], F32, "rstd_all")
    K.epsb = K.sb([128, 1], F32, "epsb")
    K.ident_bf = K.sb([128, 128], BF16, "identbf")
    K.ident32 = K.sb([128, 128], F32, "ident32")
    K.ones32 = K.sb([128, 128], F32, "ones32")
    K.persist_end = K.off
    P.pool(lambda e: e.memset(K.epsb[:], EPS), writes=["epsb"])
    P.pool(lambda e: e.memset(K.ones32[:], 1.0), writes=["ones32"])
    P.pool(lambda e: e.memset(K.ident32[:], 1.0), writes=["id32"])
    P.pool(lambda e: e.affine_select(K.ident32[:], K.ident32[:], [[-1, 128]], ALU.is_equal, 0.0,
                                     base=0, channel_multiplier=1), reads=["id32"], writes=["id32"])
    P.dve(lambda e: e.tensor_copy(K.ident_bf[:], K.ident32[:]), reads=["id32"], writes=["idbf"])


def xrow(xd, g):
    return xd[g * 128:(g + 1) * 128, :]


def stats_prologue(K, xin, xin_name):
    P = K.P
    NB = 8
    xs = [K.sb([128, D], F32, "pxs") for _ in range(NB)]
    junk = K.sb([128, D], BF16, "pjunk")
    for g in range(NSUB):
        b = g % NB
        P.dma("sp", xs[b][:], xrow(xin, g), reads=[("xd", xin_name, g)], writes=[("pxs", b)])
        P.act(_sq_accum(junk, xs[b], K.ss_all, g), reads=[("pxs", b)], writes=[("ss", g), "pjunk"])


def _sq_accum(junk, src, ss, g):
    return lambda e: e.activation(junk[:], src[:], AF.Square, accum_out=ss[:, g:g + 1])


def rstd_from_ss(K):
    P = K.P
    rd = [("ss", g) for g in range(NSUB)]
    P.act(lambda e: e.activation(K.rstd_all[:], K.ss_all[:], AF.Sqrt, bias=K.epsb[:, 0:1], scale=1.0 / D),
          reads=rd + ["epsb"], writes=["rstd_tmp"])
    P.dve(lambda e: e.reciprocal(K.rstd_all[:], K.rstd_all[:]), reads=["rstd_tmp"], writes=["rstd_all"])


def load_bcast(K, dst, src_row_ap, tok):
    n = dst.shape[-1]
    K.P.dma("sp", dst[:], src_row_ap.broadcast_to([128, n]), writes=[tok])


def make_h(K, xin, xin_name, g, xs, hb, gbc, gtok, out_dt_tag):
    P = K.P
    b = K.rot("xs", [0, 1, 2])
    P.dma("sp", xs[b][:], xrow(xin, g), reads=[("xd", xin_name, g)], writes=[("xs", b)])
    hbuf = K.rot("hb" + out_dt_tag, [0, 1])
    P.dve(lambda e: e.scalar_tensor_tensor(hb[hbuf][:], xs[b][:], K.rstd_all[:, g:g + 1], gbc[:],
                                           ALU.mult, ALU.mult),
          reads=[("xs", b), "rstd_all", gtok], writes=[("hb" + out_dt_tag, hbuf)])
    return hbuf


def transpose_bf(K, hb, hbuf, hT, s, bank):
    P = K.P
    pst = K.ps[bank][:].bitcast(BF16)
    for c in range(8):
        P.pe(_tr(pst[:, c * 128:(c + 1) * 128], hb[hbuf][:, c * 128:(c + 1) * 128], K.ident_bf),
             reads=[("hbb", hbuf), "idbf"], writes=[("ps", bank)])
    src = pst.rearrange("p (c t) -> p c t", c=8)
    dst = hT[:, :, s * 128:(s + 1) * 128]
    P.act(lambda e: e.activation(dst, src, AF.Copy), reads=[("ps", bank)], writes=[("hT", s)])


def _tr(out, in_, ident):
    return lambda e: e.transpose(out, in_, ident[:])


NPC = 6


def ffn_weight_pieces(K, l, wup, wdn):
    P = K.P
    w_up_v = K.inp("w_up", [2, D, 2 * DFF])[l].rearrange("(k p) n -> p k n", p=128)
    w_dn_v = K.inp("w_down", [2, DFF, D])[l].rearrange("(f p) n -> p f n", p=128)
    out = []
    for i in range(NPC):
        c0 = i * 512
        c1 = min(DFF, c0 + 512)
        for base in (0, DFF):
            for kh in range(2):
                ks = slice(kh * 4, kh * 4 + 4)
                out.append(lambda base=base, c0=c0, c1=c1, i=i, kh=kh, ks=ks: P.dma(
                    "pool", wup[:, ks, base + c0:base + c1], w_up_v[:, ks, base + c0:base + c1],
                    writes=[("wup", base, i, kh)]))
    for f2 in range(NF // 2):
        out.append(lambda f2=f2: P.dma("pool", wdn[:, 2 * f2:2 * f2 + 2, :], w_dn_v[:, 2 * f2:2 * f2 + 2, :],
                                       writes=[("wdn", f2)]))
    return out


def ffn_alloc_weights(K):
    wup = K.sb([128, 8, 2 * DFF], BF16, "wup")
    wdn = K.sb([128, NF, D], BF16, "wdn")
    return wup, wdn


def phase_ffn(K, l, xin, xin_name, xout, xout_name, pre=None, fuse_final=False):
    nc, P = K.nc, K.P
    K.off = K.persist_end
    K.rot_state = {}
    fvec_d = K.inp("ffn_vec", [2, 128, NF * 4])[l]
    gain_d = K.inp("norm_ffn", [2, 1, D])[l]
    if pre is None:
        wup, wdn = ffn_alloc_weights(K)
    else:
        wup, wdn = K.prefetched
        K.off = K.prefetched_end
    gbc = K.sb([128, D], F32, "gbc")
    fvec = K.sb([128, NF, 4], F32, "fvec")
    halo = K.sb([128, NF, 2], F32, "halo")
    xs = [K.sb([128, D], F32, "xs") for _ in range(3)]
    hb = [K.sb([128, D], BF16, "hb") for _ in range(2)]
    hT = K.sb([128, 8, T], BF16, "hT")
    gT = K.sb([128, NF, T], BF16, "gT")
    ub = [K.sb([128, T + 2], F32, "ub") for _ in range(2)]
    acc = [K.sb([128, T], F32, "acc") for _ in range(2)]
    tt = [K.sb([128, T], F32, "tt") for _ in range(2)]
    junk = K.sb([128, D], BF16, "junk")
    if fuse_final:
        gfin = K.sb([128, D], F32, "gfin")
        rsf = K.sb([128, 4], F32, "rsf")
        load_bcast(K, gfin, K.inp("final_norm", [1, D]), "gfin")

    load_bcast(K, gbc, gain_d, "gbc")
    P.dma("sp", fvec[:].rearrange("p f k -> p (f k)"), fvec_d, writes=["fvec"])
    P.pool(lambda e: e.memset(halo[:], 0.0), writes=[("halo", f) for f in range(NF)])
    pend = []
    if pre is None:
        pend = ffn_weight_pieces(K, l, wup, wdn)
        for issue in pend[:8]:
            issue()
        pend = pend[8:]
    rstd_from_ss(K)

    def norm_tile(jn, subs=(0, 1, 2, 3)):
        for s in subs:
            g = jn * 4 + s
            hbuf = make_h(K, xin, xin_name, g, xs, hb, gbc, "gbc", "b")
            transpose_bf(K, hb, hbuf, hT, s, K.rot("pst", [0, 1]))

    finals = []
    norm_tile(0)
    for j in range(NT):
        hT_tok = [("hT", s) for s in range(4)]
        for f in range(NF):
            for _ in range(1 if f < 16 else 2):
                if pend:
                    pend.pop(0)()
            bu = K.rot("pu", [2, 3])
            bv = K.rot("pv", [4, 5])
            pi = f // 4
            for k in range(8):
                mm(P, K.ps[bu][:], wup[:, k, f * 128:(f + 1) * 128], hT[:, k, :], k == 0, k == 7,
                   hT_tok + [("wup", 0, pi, k // 4)], [("ps", bu)])
            for k in range(8):
                mm(P, K.ps[bv][:], wup[:, k, DFF + f * 128:DFF + (f + 1) * 128], hT[:, k, :], k == 0, k == 7,
                   hT_tok + [("wup", DFF, pi, k // 4)], [("ps", bv)])
            ui = K.rot("ub", [0, 1])
            u = ub[ui]
            a = acc[ui]
            t = tt[ui]
            P.pool(_copy(u[:, 0:2], halo[:, f, :]), reads=[("halo", f)], writes=[("ubh", ui)])
            P.act(_actcopy(u[:, 2:T + 2], K.ps[bu][:]), reads=[("ps", bu)], writes=[("ub", ui)])
            P.pool(_copy(halo[:, f, :], u[:, T:T + 2]), reads=[("ub", ui)], writes=[("halo", f)])
            P.act(_act(a[:], K.ps[bu][:], AF.Identity, bias=fvec[:, f, 3:4], scale=fvec[:, f, 2:3]),
                  reads=[("ps", bu), "fvec"], writes=[("acc", ui)])
            P.dve(_stt(a[:], u[:, 0:T], fvec[:, f, 0:1], a[:], ALU.mult, ALU.add),
                  reads=[("ubh", ui), ("ub", ui), "fvec", ("acc", ui)], writes=[("acc", ui)])
            P.dve(_stt(a[:], u[:, 1:T + 1], fvec[:, f, 1:2], a[:], ALU.mult, ALU.add),
                  reads=[("ubh", ui), ("ub", ui), ("acc", ui)], writes=[("acc", ui)])
            P.act(_tanh_half(t[:], a[:]), reads=[("acc", ui)], writes=[("tt", ui)])
            P.dve(_stt(t[:], t[:], 1.0, a[:], ALU.add, ALU.mult), reads=[("tt", ui), ("acc", ui)],
                  writes=[("tt", ui)])
            P.dve(_stt(gT[:, f, :], t[:], 0.5, K.ps[bv][:], ALU.mult, ALU.mult),
                  reads=[("tt", ui), ("ps", bv)], writes=[("gT", f)])
        while pend:
            pend.pop(0)()
        gT_tok = [("gT", f) for f in range(NF)]
        for s in range(4):
            if s == 0 and j + 1 < NT:
                norm_tile(j + 1)
            g = j * 4 + s
            b = K.rot("xs", [0, 1, 2])
            P.dma("sp", xs[b][:], xrow(xin, g), reads=[("xd", xin_name, g)], writes=[("xs", b)])
            for half in range(2):
                by = K.rot("py", [6, 7])
                for f in range(NF):
                    mm(P, K.ps[by][:], gT[:, f, s * 128:(s + 1) * 128], wdn[:, f, half * 512:(half + 1) * 512],
                       f == 0, f == NF - 1, [("gT", f), ("wdn", f // 2)], [("ps", by)])
                P.dve(_tt(xs[b][:, half * 512:(half + 1) * 512], xs[b][:, half * 512:(half + 1) * 512],
                          K.ps[by][:], ALU.add), reads=[("ps", by), ("xs", b)], writes=[("xs", b)])
            P.act(_sq_accum(junk, xs[b], K.ss_all, g), reads=[("xs", b)], writes=[("ss", g), "junk"])
            if fuse_final:
                ri = s
                P.act(_act(rsf[:, ri:ri + 1], K.ss_all[:, g:g + 1], AF.Sqrt, bias=K.epsb[:, 0:1], scale=1.0 / D),
                      reads=[("ss", g), "epsb"], writes=[("rsf", ri)])
                P.dve(_recip(rsf[:, ri:ri + 1], rsf[:, ri:ri + 1]), reads=[("rsf", ri)], writes=[("rsf", ri)])
                P.dve(_stt(xs[b][:], xs[b][:], rsf[:, ri:ri + 1], gfin[:], ALU.mult, ALU.mult),
                      reads=[("xs", b), ("rsf", ri), "gfin"], writes=[("xs", b)])
            finals.append(P.dma("pool", xrow(xout, g), xs[b][:], reads=[("xs", b)], writes=[("xd", xout_name, g)]))
    return finals


def _copy(out, in_):
    return lambda e: e.tensor_copy(out, in_)


def _actcopy(out, in_):
    return lambda e: e.activation(out, in_, AF.Copy)


def _ts2(out, in0, s1, s2):
    return lambda e: e.tensor_scalar(out, in0, s1, s2, ALU.mult, ALU.add)


def _stt(out, in0, sc, in1, op0, op1):
    return lambda e: e.scalar_tensor_tensor(out, in0, sc, in1, op0, op1)


def _tt(out, in0, in1, op):
    return lambda e: e.tensor_tensor(out, in0, in1, op)


def _tanh_half(out, in_):
    return lambda e: e.activation(out, in_, AF.Tanh, scale=0.5)


def phase_final(K, xin, xin_name, xout, xout_name):
    P = K.P
    K.off = K.persist_end
    K.rot_state = {}
    gain_d = K.inp("final_norm", [1, D])
    gbc = K.sb([128, D], F32, "gbc")
    xs = [K.sb([128, D], F32, "xs") for _ in range(3)]
    ob = [K.sb([128, D], F32, "ob") for _ in range(2)]
    load_bcast(K, gbc, gain_d, "gbc")
    rstd_from_ss(K)
    finals = []
    for g in range(NSUB):
        b = K.rot("xs", [0, 1, 2])
        o = K.rot("ob", [0, 1])
        P.dma("sp", xs[b][:], xrow(xin, g), reads=[("xd", xin_name, g)], writes=[("xs", b)])
        P.dve(_stt(ob[o][:], xs[b][:], K.rstd_all[:, g:g + 1], gbc[:], ALU.mult, ALU.mult),
              reads=[("xs", b), "rstd_all", "gbc"], writes=[("ob", o)])
        finals.append(P.dma("pool", xrow(xout, g), ob[o][:], reads=[("ob", o)], writes=[("xd", xout_name, g)]))
    return finals


def phase_mix1(K, xin, xin_name, xout, xout_name, prefetch_ffn=None):
    P = K.P
    K.off = K.persist_end
    K.rot_state = {}
    HL = 16
    gain_d = K.inp("norm_mix_o", [1, D])
    psc_d = K.inp("pool_scale", [1, D])
    pw_d = K.inp("pool_w", [4, 256, 256])
    wpieces = []
    if prefetch_ffn is not None:
        wup_n, wdn_n = ffn_alloc_weights(K)
        K.prefetched = (wup_n, wdn_n)
        K.prefetched_end = K.off
        wpieces = ffn_weight_pieces(K, prefetch_ffn, wup_n, wdn_n)
    gbc = K.sb([128, D], F32, "gbc")
    pw = K.sb([128, 4, 2, 256], BF16, "pw")
    xs = [K.sb([128, D], F32, "xs") for _ in range(3)]
    hb = [K.sb([128, D], F32, "hb32") for _ in range(2)]
    hT = K.sb([128, 8, HL + T], F32, "hT32")
    sA0 = K.sb([128, 2, HL + T], F32, "sA0")
    sa0_off = K.last_off
    sB0 = K.sb([128, 2, HL + T], F32, "sB0")
    sA1 = K.sb([128, 2, HL + T], F32, "sA1")
    sa1_off = K.last_off
    sB1 = K.sb([128, 2, HL + T], F32, "sB1")
    pw32 = K.sb_at([128, 4, 2, 256], F32, "pw32", sa0_off)
    psc = K.sb_at([128, D], F32, "psc", sa1_off)
    pT = K.sb([128, 8, T], BF16, "pT")
    ic = K.sb([128, 4, 16], F32, "ic")
    ici = K.sb([128, 16], F32, "ici")
    junk = K.sb([128, D], BF16, "junk")
    W = (2, 4, 8, 16)

    load_bcast(K, gbc, gain_d, "gbc")
    load_bcast(K, psc, psc_d, "psc")
    for gi in range(4):
        P.dma("sp", pw32[:, gi, :, :], pw_d[gi].rearrange("(k p) n -> p k n", p=128), writes=[("pw32", gi)])
    for gi in range(4):
        for kc in range(2):
            P.dve(_tt(pw[:, gi, kc, :], pw32[:, gi, kc, :], psc[:, gi * 256:(gi + 1) * 256], ALU.mult),
                  reads=[("pw32", gi), "psc"], writes=[("pw", gi), "sA0", "sB0", "sA1"])
    P.pool(lambda e: e.iota(ici[:], [[1, 16]], base=1, channel_multiplier=0, allow_small_or_imprecise_dtypes=True), writes=["ici"])
    for gi in range(4):
        P.dve(_tsmin(ic[:, gi, :], ici[:], float(W[gi])), reads=["ici"], writes=[("ic", gi)])
        P.dve(_recip(ic[:, gi, :], ic[:, gi, :]), reads=[("ic", gi)], writes=[("ic", gi)])
    P.pool(lambda e: e.memset(hT[:, :, 0:HL], 0.0), writes=[("hTh", c) for c in range(8)])
    rstd_from_ss(K)

    def tileA(j):
            for s in range(4):
                g = j * 4 + s
                hbuf = make_h(K, xin, xin_name, g, xs, hb, gbc, "gbc", "f")
                for hf in range(2):
                    bank = K.rot("pst", [0, 1, 2, 3])
                    for c4 in range(4):
                        c = hf * 4 + c4
                        P.pe(_tr(K.ps[bank][:, c4 * 128:(c4 + 1) * 128], hb[hbuf][:, c * 128:(c + 1) * 128], K.ident32),
                             reads=[("hbf", hbuf), "id32"], writes=[("ps", bank)])
                    src = K.ps[bank][:].rearrange("p (c t) -> p c t", c=4)
                    dst = hT[:, hf * 4:(hf + 1) * 4, HL + s * 128:HL + (s + 1) * 128]
                    P.act(_actcopy(dst, src), reads=[("ps", bank)],
                          writes=[("hT", hf * 4 + c4, s) for c4 in range(4)])

    def tileB(j):
            for gi in range(4):
                cs = slice(2 * gi, 2 * gi + 2)
                hdeps = [("hT", c, s) for c in (2 * gi, 2 * gi + 1) for s in range(4)] + \
                        [("hTh", c) for c in (2 * gi, 2 * gi + 1)]
                w = W[gi]
                L = HL + T
                weng = "dve"
                sA, sB = (sA0, sB0) if gi < 2 else (sA1, sB1)
                tA, tB = ("sA0", "sB0") if gi < 2 else ("sA1", "sB1")
                P.add(weng, _tt(sA[:, :, 1:L], hT[:, cs, 1:L], hT[:, cs, 0:L - 1], ALU.add),
                      reads=hdeps, writes=[tA])
                cur, curtok, other, othertok = sA, tA, sB, tB
                sh = 2
                while sh < w:
                    P.add(weng, _tt(other[:, :, 2 * sh - 1:L], cur[:, :, 2 * sh - 1:L], cur[:, :, sh - 1:L - sh], ALU.add),
                          reads=[curtok], writes=[othertok])
                    cur, curtok, other, othertok = other, othertok, cur, curtok
                    sh *= 2
                ptok = [("pT", c) for c in (2 * gi, 2 * gi + 1)]
                if j == 0:
                    for c2 in range(2):
                        c = 2 * gi + c2
                        P.dve(_tt(cur[:, c2, HL:HL + 16], cur[:, c2, HL:HL + 16], ic[:, gi, :], ALU.mult),
                              reads=[curtok, ("ic", gi)], writes=[curtok])
                        P.dve(_tt(pT[:, c, 0:16], cur[:, c2, HL:HL + 16], hT[:, c, HL:HL + 16], ALU.subtract),
                              reads=[curtok] + hdeps, writes=[("pT", c)])
                    P.dve(_stt(pT[:, cs, 16:T], cur[:, :, HL + 16:L], 1.0 / w, hT[:, cs, HL + 16:L],
                               ALU.mult, ALU.subtract), reads=[curtok] + hdeps + ptok, writes=ptok)
                else:
                    P.dve(_stt(pT[:, cs, :], cur[:, :, HL:L], 1.0 / w, hT[:, cs, HL:L], ALU.mult, ALU.subtract),
                          reads=[curtok] + hdeps, writes=ptok)
                P.act(_actcopy(hT[:, cs, 0:HL], hT[:, cs, T:T + HL]), reads=hdeps,
                      writes=[("hTh", c) for c in (2 * gi, 2 * gi + 1)])

    def tileC(j):
            for s in range(4):
                g = j * 4 + s
                b = K.rot("xs", [0, 1, 2])
                P.dma("sp", xs[b][:], xrow(xin, g), reads=[("xd", xin_name, g)], writes=[("xs", b)])
                for half in range(2):
                    by = K.rot("py", [4, 5, 6, 7])
                    for g2 in range(2):
                        gi = half * 2 + g2
                        for kc in range(2):
                            c = 2 * gi + kc
                            mm(P, K.ps[by][:, g2 * 256:(g2 + 1) * 256], pT[:, c, s * 128:(s + 1) * 128],
                               pw[:, gi, kc, :], kc == 0, kc == 1, [("pT", c), ("pw", gi)], [("ps", by)])
                    hs = slice(half * 512, (half + 1) * 512)
                    P.dve(_tt(xs[b][:, hs], xs[b][:, hs], K.ps[by][:], ALU.add), reads=[("ps", by), ("xs", b)],
                          writes=[("xs", b)])
                P.act(_sq_accum(junk, xs[b], K.ss_all, g), reads=[("xs", b)], writes=[("ss", g), "junk"])
                P.dma("sp", xrow(xout, g), xs[b][:], reads=[("xs", b)], writes=[("xd", xout_name, g)])

    tileA(0)
    for j in range(NT):
        for issue in (wpieces[j * 5:(j + 1) * 5] if j < NT - 1 else wpieces[j * 5:]):
            issue()
        tileB(j)
        if j + 1 < NT:
            tileA(j + 1)
        tileC(j)


def _tsmin(out, in_, v):
    return lambda e: e.tensor_scalar(out, in_, v, None, ALU.min)


def _recip(out, in_):
    return lambda e: e.reciprocal(out, in_)


PHASES_ALL = ("mix0", "ffn0", "mix1", "ffn1f")


def build(phases, need_stats):
    nc = bass.Bass("TRN2", target_bir_lowering=False)
    K = Ctx(nc)
    P = K.P
    xin = K.inp("xin", [S, D])
    xout = nc.dram_tensor("xout", [S, D], F32, kind="ExternalOutput").ap()
    setup_common(K)
    bufs = {}
    n = len(phases)
    cur, cur_name = xin, "xin"
    if n > 1:
        scratch = nc.dram_tensor("xscr", [S, D], F32).ap()
    if need_stats:
        stats_prologue(K, xin, "xin")
    else:
        ss_d = K.inp("ss_in", [128, NSUB])
        P.dma("sp", K.ss_all[:], ss_d, writes=[("ss", g) for g in range(NSUB)])
    finals = None
    for i, ph in enumerate(phases):
        last = i == n - 1
        dst, dst_name = (xout, "xout") if last else (scratch, "xscr")
        P.barrier()
        if ph == "mix0":
            from_mix0 = phase_mix0(K, cur, cur_name, dst, dst_name)
        elif ph == "ffn0":
            phase_ffn(K, 0, cur, cur_name, dst, dst_name)
        elif ph == "mix1":
            nxt = phases[i + 1] if i + 1 < n else None
            phase_mix1(K, cur, cur_name, dst, dst_name, prefetch_ffn=1 if nxt in ("ffn1", "ffn1f") else None)
        elif ph == "ffn1":
            phase_ffn(K, 1, cur, cur_name, dst, dst_name, pre=(i > 0 and phases[i - 1] == "mix1") or None)
        elif ph == "ffn1f":
            finals = phase_ffn(K, 1, cur, cur_name, dst, dst_name, pre=(i > 0 and phases[i - 1] == "mix1") or None,
                               fuse_final=True)
        elif ph == "final":
            finals = phase_final(K, cur, cur_name, dst, dst_name)
        cur, cur_name = dst, dst_name
    ss_out = nc.dram_tensor("ss_out", [128, NSUB], F32, kind="ExternalOutput").ap()
    P.barrier()
    f2 = P.dma("sp", ss_out, K.ss_all[:])
    P.barrier()
    P.finish([f2])
    P.emit()
    return nc, K


A0, G0, CQ0, CKV0, KR0, KRS0, WIN_COLS = 0, 512, 1024, 1408, 1664, 1760, 1856
MV_CW, MV_CB, MV_LG, MV_LB, MV_GQ, MV_GKV, MV_INVF, MV_SGN, MV_COLS = 0, 124, 128, 132, 136, 139, 141, 142, 143


def _act(out, in_, func, **kw):
    return lambda e: e.activation(out, in_, func, **kw)


def _ts1(out, in0, s1, op0):
    return lambda e: e.tensor_scalar(out, in0, s1, None, op0)


def _ts(out, in0, s1, s2, op0, op1):
    return lambda e: e.tensor_scalar(out, in0, s1, s2, op0, op1)


def phase_mix0(K, xin, xin_name, xout, xout_name):
    nc, P = K.nc, K.P
    K.off = K.persist_end
    K.rot_state = {}
    w_in_d = K.inp("w_in_ext", [D, WIN_COLS])
    w_uq_d = K.inp("w_uq_ext", [384, 8 * 192])
    w_kn_d = K.inp("w_kn", [256, 512])
    w_v_d = K.inp("w_v", [256, 512])
    w_out_d = K.inp("w_out", [D, D])
    mvec_d = K.inp("mix_vec", [128, MV_COLS])
    gain_d = K.inp("norm_mix_e", [1, D])
    pos_d = K.inp("pos", [1, S], I32)
    kt_d = nc.dram_tensor("kt_d", [8, NT, 96, 512], BF16).ap()
    v_d = nc.dram_tensor("v_d", [8, NT, 128, 260], BF16).ap()
    cos_d = nc.dram_tensor("cos_d", [32, S], F32).ap()
    sin_d = nc.dram_tensor("sin_d", [32, S], F32).ap()

    w_in = K.sb([128, 8, WIN_COLS], BF16, "w_in")
    w_uq = K.sb([128, 3, 8 * 192], BF16, "w_uq")
    w_kn = K.sb([128, 2, 512], BF16, "w_kn")
    w_v = K.sb([128, 2, 512], BF16, "w_v")
    w_oc = K.sb([128, 4, D], BF16, "w_oc")
    w_oh = K.sb([128, 4, D], BF16, "w_oh")
    gbc = K.sb([128, D], F32, "gbc")
    mvec = K.sb([128, MV_COLS], F32, "mvec")
    wch = K.sb([128, 4, 31], F32, "wch")
    lnh = K.sb([128, 8], F32, "lnh")
    xs = [K.sb([128, D], F32, "xs") for _ in range(3)]
    hb = [K.sb([128, D], BF16, "hb") for _ in range(2)]
    HO = K.sb([128, 8, T], BF16, "HO")
    junk = K.sb([128, D], BF16, "junk")
    up = K.sb([128, 4, 30 + T], BF16, "up")
    DGp = K.sb([128, 124, 128], BF16, "DGp")
    cv = K.sb([128, 4, T], F32, "cv")
    cqg = K.sb([128, 3, T], BF16, "cqg")
    ckvg = K.sb([128, 2, T], BF16, "ckvg")
    Fb = [K.sb([128, T], F32, "F") for _ in range(8)]
    RQ = K.sb([128, T], F32, "RQ")
    RKV = K.sb([128, T], F32, "RKV")
    RKT = K.sb([128, 4], F32, "RKT")
    CS = K.sb([128, 2, T], F32, "CS")
    CR = K.sb([128, 2, T], F32, "CR")
    RD = Fb[7]
    QT = K.sb([128, 8, T], BF16, "QT")
    KTs = K.sb([128, 8, T], BF16, "KTs")
    Vs = K.sb([128, 8, 4, 65], BF16, "Vs")
    KTc = [K.sb([128, T], BF16, "KTc") for _ in range(4)]
    Vc = [K.sb([128, 4, 65], BF16, "Vc") for _ in range(4)]
    PT = [K.sb([128, T], BF16, "PT") for _ in range(4)]
    mask32 = Fb[7]
    mask = K.sb([128, 128], BF16, "mask")
    MEAN = K.sb([128, T], F32, "MEAN")
    RLN = K.sb([128, T], F32, "RLN")
    UT = K.sb([128, 4, T], BF16, "UT")
    hT = HO
    oT = HO

    def Fnew():
        i = K.rot("F", list(range(7)))
        return i, Fb[i]

    P.dma("sp", mvec[:], mvec_d, writes=["mvec"])
    load_bcast(K, gbc, gain_d, "gbc")
    P.pool(lambda e: e.memset(Vs[:], 1.0), writes=[("Vs", s) for s in range(4)])
    P.pool(lambda e: e.memset(up[:, :, 0:30], 0.0), writes=[("u", c) for c in range(4)])
    P.pool(lambda e: e.memset(mask32[:, 0:128], 1.0), writes=["mask32"])
    P.pool(lambda e: e.affine_select(mask32[:, 0:128], mask32[:, 0:128], [[1, 128]], ALU.is_ge, 0.0,
                                     base=0, channel_multiplier=-1), reads=["mask32"], writes=["mask32"])
    P.dve(_copy(mask[:], mask32[:, 0:128]), reads=["mask32"], writes=["mask", "RD"])
    w_in_v = w_in_d.rearrange("(k p) n -> p k n", p=128)
    for i, (c0, c1) in enumerate(((0, 512), (512, 1024), (1024, 1408), (1408, WIN_COLS))):
        for kh in range(2):
            P.dma("pool", w_in[:, kh * 4:kh * 4 + 4, c0:c1], w_in_v[:, kh * 4:kh * 4 + 4, c0:c1],
                  writes=[("w_in", i, kh)])
    w_uq_v = w_uq_d.rearrange("(k p) n -> p k n", p=128)
    for k in range(3):
        P.dma("pool", w_uq[:, k, :], w_uq_v[:, k, :], writes=[("w_uq", k)])
    P.dma("pool", w_kn[:], w_kn_d.rearrange("(k p) n -> p k n", p=128), writes=["w_kn"])
    P.dma("pool", w_v[:], w_v_d.rearrange("(k p) n -> p k n", p=128), writes=["w_v"])
    w_oc_v = w_out_d[0:512, :].rearrange("(c p) n -> p c n", p=128)
    for c in range(4):
        P.dma("pool", w_oc[:, c, :], w_oc_v[:, c, :], writes=["w_oc%d" % c])
    w_oh_v = w_out_d[512:1024, :].rearrange("(q p) n -> p q n", p=128)
    for h2 in range(4):
        P.dma("pool", w_oh[:, h2, :], w_oh_v[:, h2, :], writes=["w_oh%d" % h2])
    P.dve(_ts1(wch[:].rearrange("p c k -> p (c k)"), mvec[:, MV_CW:MV_CW + 124], 0.5, ALU.mult),
          reads=["mvec"], writes=["wch"])
    P.dve(_ts1(lnh[:], mvec[:, MV_LG:MV_LG + 8], 0.5, ALU.mult), reads=["mvec"], writes=["lnh"])
    R = slice(64, 96)
    posi = Fb[0][:].bitcast(I32)
    ki = Fb[1][:].bitcast(I32)
    posf, ang, a2, xk, kf = Fb[2], Fb[3], Fb[4], Fb[5], Fb[6]
    negpi = float(-np.pi)

    def rope_dve(ch):
        cs_ = slice(ch * T, (ch + 1) * T)
        P.dma("sp", posi[R, :], pos_d[0:1, cs_].broadcast_to([32, T]), writes=[("F", 0)])
        P.dve(_copy(posf[R, :], posi[R, :]), reads=[("F", 0)], writes=[("F", 2)])
        P.dve(_ts1(ang[R, :], posf[R, :], mvec[R, MV_INVF:MV_INVF + 1], ALU.mult), reads=[("F", 2), "mvec"],
              writes=[("F", 3)])
        for which in range(2):
            if which == 1:
                P.dve(_ts1(a2[R, :], ang[R, :], float(np.pi / 2), ALU.add), reads=[("F", 3)], writes=[("F", 4)])
                src, stok = a2, ("F", 4)
            else:
                src, stok = ang, ("F", 3)
            P.dve(_ts1(xk[R, :], src[R, :], float(1.0 / TWO_PI), ALU.mult), reads=[stok], writes=[("F", 5)])
            P.dve(_copy(ki[R, :], xk[R, :]), reads=[("F", 5)], writes=[("F", 1)])
            P.dve(_copy(kf[R, :], ki[R, :]), reads=[("F", 1)], writes=[("F", 6)])
            P.dve(_stt(xk[R, :], kf[R, :], -C1, src[R, :], ALU.mult, ALU.add), reads=[("F", 6), stok],
                  writes=[("F", 5)])
            P.dve(_stt(xk[R, :], kf[R, :], -C2, xk[R, :], ALU.mult, ALU.add), reads=[("F", 6), ("F", 5)],
                  writes=[("F", 5)])
            P.dve(_ts(CS[R, 1 - which, :], xk[R, :], negpi, -negpi, ALU.max, ALU.min), reads=[("F", 5)],
                  writes=["cs%d" % (1 - which)])

    def rope_act():
        P.act(_act(CS[R, 1, :], CS[R, 1, :], AF.Sin, scale=mvec[R, MV_SGN:MV_SGN + 1]),
              reads=["cs1", "mvec"], writes=["cs1"])
        P.act(_act(CS[R, 0, :], CS[R, 0, :], AF.Sin), reads=["cs0"], writes=["cs0"])

    def rope_tables(ch):
        rope_dve(ch)
        rope_act()

    rope_tables(0)
    rstd_from_ss(K)

    HOB = "HObuf"
    kv_base = 0
    for j in range(NT):
        ts_ = slice(j * T, (j + 1) * T)
        for s in range(4):
            g = j * 4 + s
            hbuf = make_h(K, xin, xin_name, g, xs, hb, gbc, "gbc", "b")
            bank = K.rot("pA", [0, 1, 2, 3])
            pst = K.ps[bank][:].bitcast(BF16)
            for c in range(8):
                P.pe(_tr(pst[:, c * 128:(c + 1) * 128], hb[hbuf][:, c * 128:(c + 1) * 128], K.ident_bf),
                     reads=[("hbb", hbuf), "idbf"], writes=[("ps", bank)])
            P.act(_actcopy(hT[:, :, s * 128:(s + 1) * 128], pst.rearrange("p (c t) -> p c t", c=8)),
                  reads=[("ps", bank)], writes=[HOB])
        hrd = [HOB]

        def proj(col0, M, wtok):
            bank = K.rot("pA", [0, 1, 2, 3])
            for k in range(8):
                mm(P, K.ps[bank][0:M, :], w_in[:, k, col0:col0 + M], hT[:, k, :], k == 0, k == 7,
                   hrd + [wtok + (k // 4,)], [("ps", bank)])
            return bank

        def conv_chunk(c):
            bank = K.rot("pA", [0, 1, 2, 3])
            for k in range(31):
                mm(P, K.ps[bank][:], DGp[:, c * 31 + k, :], up[:, c, k:k + T], k == 0, k == 30,
                   [("DGp", c), ("u", c)], [("ps", bank)])
            P.dve(_ts1(cv[:, c, :], K.ps[bank][:], mvec[:, MV_CB + c:MV_CB + c + 1], ALU.add),
                  reads=[("ps", bank), "mvec"], writes=[("cv", c)])

        for c in range(4):
            ba = proj(A0 + c * 128, 128, ("w_in", 0))
            bg = proj(G0 + c * 128, 128, ("w_in", 1))
            fi, tg = Fnew()
            P.act(_tanh_half(tg[:], K.ps[bg][:]), reads=[("ps", bg)], writes=[("F", fi)])
            P.dve(_stt(up[:, c, 30:30 + T], tg[:], 1.0, K.ps[ba][:], ALU.add, ALU.mult),
                  reads=[("F", fi), ("ps", ba)], writes=[("u", c)])
        sq_kv = []
        for c in range(3):
            bk = proj(CQ0 + c * 128, 128, ("w_in", 2))
            P.act(_act(cqg[:, c, :], K.ps[bk][:], AF.Copy, scale=mvec[:, MV_GQ + c:MV_GQ + c + 1]),
                  reads=[("ps", bk), "mvec"], writes=[("cqg", c)])
            fi, sq = Fnew()
            P.act(_act(sq[:], K.ps[bk][:], AF.Square), reads=[("ps", bk)], writes=[("F", fi)])
            mm(P, K.ps[6][:], K.ones32[:], sq[:], c == 0, c == 2, ["ones32", ("F", fi)], [("ps", 6)])
        for c in range(2):
            bk = proj(CKV0 + c * 128, 128, ("w_in", 3))
            P.act(_act(ckvg[:, c, :], K.ps[bk][:], AF.Copy, scale=mvec[:, MV_GKV + c:MV_GKV + c + 1]),
                  reads=[("ps", bk), "mvec"], writes=[("ckvg", c)])
            fi, sq = Fnew()
            P.act(_act(sq[:], K.ps[bk][:], AF.Square), reads=[("ps", bk)], writes=[("F", fi)])
            mm(P, K.ps[7][:], K.ones32[:], sq[:], c == 0, c == 1, ["ones32", ("F", fi)], [("ps", 7)])
            sq_kv.append((fi, sq))
        bkt = K.rot("pA", [0, 1, 2, 3])
        for s in range(4):
            for c in range(2):
                fi, sq = sq_kv[c]
                mm(P, K.ps[bkt][:, s:s + 1], sq[:, s * 128:(s + 1) * 128], K.ones32[:, 0:1], c == 0, c == 1,
                   ["ones32", ("F", fi)], [("ps", bkt)])
        bkr = proj(KR0, 96, ("w_in", 3))
        bkrs = proj(KRS0, 96, ("w_in", 3))
        f1, t1 = Fnew()
        f2, t2 = Fnew()
        P.dve(_tt(t1[R, :], K.ps[bkr][R, :], CS[R, 0, :], ALU.mult), reads=[("ps", bkr), "cs0"], writes=[("F", f1)])
        P.dve(_tt(t2[R, :], K.ps[bkrs][R, :], CS[R, 1, :], ALU.mult), reads=[("ps", bkrs), "cs1"], writes=[("F", f2)])
        P.dve(_tt(KTs[R, 0, :], t1[R, :], t2[R, :], ALU.add), reads=[("F", f1), ("F", f2)], writes=[("KTr", 0)])
        for h in range(1, 8):
            P.dve(_copy(KTs[R, h, :], KTs[R, 0, :]), reads=[("KTr", 0)], writes=[("KTr", h)])
        P.act(_act(RQ[:], K.ps[6][:], AF.Sqrt, bias=K.epsb[:, 0:1], scale=1.0 / 384), reads=[("ps", 6), "epsb"],
              writes=["RQ"])
        P.act(_act(RKV[:], K.ps[7][:], AF.Sqrt, bias=K.epsb[:, 0:1], scale=1.0 / 256), reads=[("ps", 7), "epsb"],
              writes=["RKV"])
        P.act(_act(RKT[:], K.ps[bkt][:, 0:4], AF.Sqrt, bias=K.epsb[:, 0:1], scale=1.0 / 256),
              reads=[("ps", bkt), "epsb"], writes=["RKT"])
        P.dve(_recip(RQ[:], RQ[:]), reads=["RQ"], writes=["RQ"])
        P.dve(_recip(RKV[:], RKV[:]), reads=["RKV"], writes=["RKV"])
        P.dve(_recip(RKT[:], RKT[:]), reads=["RKT"], writes=["RKT"])
        P.dve(_tt(CR[R, 0, :], CS[R, 0, :], RQ[R, :], ALU.mult), reads=["cs0", "RQ"], writes=["cr0"])
        P.dve(_tt(CR[R, 1, :], CS[R, 1, :], RQ[R, :], ALU.mult), reads=["cs1", "RQ"], writes=["cr1"])
        cq_tok = [("cqg", c) for c in range(3)]
        for h in range(8):
            bq = K.rot("pA", [0, 1, 2, 3])
            for c in range(3):
                mm(P, K.ps[bq][0:96, :], w_uq[:, c, h * 192:h * 192 + 96], cqg[:, c, :], c == 0, c == 2,
                   cq_tok + [("w_uq", c)], [("ps", bq)])
            bqs = K.rot("pA", [0, 1, 2, 3])
            for c in range(3):
                mm(P, K.ps[bqs][0:96, :], w_uq[:, c, h * 192 + 96:h * 192 + 192], cqg[:, c, :], c == 0, c == 2,
                   cq_tok + [("w_uq", c)], [("ps", bqs)])
            P.dve(_tt(QT[0:64, h, :], K.ps[bq][0:64, :], RQ[0:64, :], ALU.mult), reads=[("ps", bq), "RQ"],
                  writes=[("QTn", h)])
            f1, t1 = Fnew()
            f2, t2 = Fnew()
            P.dve(_tt(t1[R, :], K.ps[bq][R, :], CR[R, 0, :], ALU.mult), reads=[("ps", bq), "cr0"], writes=[("F", f1)])
            P.dve(_tt(t2[R, :], K.ps[bqs][R, :], CR[R, 1, :], ALU.mult), reads=[("ps", bqs), "cr1"],
                  writes=[("F", f2)])
            P.dve(_tt(QT[R, h, :], t1[R, :], t2[R, :], ALU.add), reads=[("F", f1), ("F", f2)], writes=[("QTr", h)])
        ckv_tok = [("ckvg", c) for c in range(2)]
        for h in range(8):
            bk = K.rot("pA", [0, 1, 2, 3])
            for c in range(2):
                mm(P, K.ps[bk][0:64, :], w_kn[:, c, h * 64:(h + 1) * 64], ckvg[:, c, :], c == 0, c == 1,
                   ckv_tok + ["w_kn"], [("ps", bk)])
            P.dve(_tt(KTs[0:64, h, :], K.ps[bk][0:64, :], RKV[0:64, :], ALU.mult), reads=[("ps", bk), "RKV"],
                  writes=[("KTn", h)])
        for s in range(4):
            bv = K.rot("pA", [0, 1, 2, 3])
            for c in range(2):
                mm(P, K.ps[bv][:], ckvg[:, c, s * 128:(s + 1) * 128], w_v[:, c, :], c == 0, c == 1,
                   ckv_tok + ["w_v"], [("ps", bv)])
            P.act(_act(Vs[:, :, s, 0:64], K.ps[bv][:].rearrange("p (h e) -> p h e", h=8), AF.Copy,
                       scale=RKT[:, s:s + 1]), reads=[("ps", bv), "RKT"], writes=[("Vs", s)])
        P.dma("pool", kt_d[:, j].rearrange("h p t -> p h t"), KTs[0:96, :, :],
              reads=[("KTn", h) for h in range(8)] + [("KTr", h) for h in range(8)], writes=[("ktd", j)])
        P.dma("pool", v_d[:, j].rearrange("h p e -> p h e"), Vs[:].rearrange("p h b e -> p h (b e)"),
              reads=[("Vs", s) for s in range(4)], writes=[("vd", j)])
        if j == 0:
            for c in range(4):
                for k in range(31):
                    P.dve(_ts1(DGp[:, c * 31 + k, :], K.ident_bf[:], wch[:, c, k:k + 1], ALU.mult),
                          reads=["idbf", "wch"], writes=[("DGp", c)])
        vs_tok = [("Vs", s) for s in range(4)]
        chunks = [(h_, jp_) for h_ in range(8) for jp_ in range(j)]
        issued = [0]

        def ensure(n):
            while issued[0] < min(n, len(chunks)):
                h_, jp_ = chunks[issued[0]]
                cb_ = (kv_base + issued[0]) % 4
                P.dma("sp", KTc[cb_][0:96, :], kt_d[h_, jp_], reads=[("ktd", jp_)], writes=[("KTc", cb_)])
                P.dma("sp", Vc[cb_][:].rearrange("p b e -> p (b e)"), v_d[h_, jp_], reads=[("vd", jp_)],
                      writes=[("Vc", cb_)])
                issued[0] += 1

        ln_state = {}

        def ln_a():
            ln_state["sq"] = []
            for c in range(4):
                fi, sq = Fnew()
                P.dve(_tt(sq[:], cv[:, c, :], cv[:, c, :], ALU.mult), reads=[("cv", c)], writes=[("F", fi)])
                ln_state["sq"].append((fi, sq))
            for c in range(4):
                mm(P, K.ps[7][:], K.ones32[:], cv[:, c, :], c == 0, c == 3, ["ones32", ("cv", c)], [("ps", 7)])

        def ln_b():
            P.dve(_ts1(MEAN[:], K.ps[7][:], 1.0 / 512, ALU.mult), reads=[("ps", 7)], writes=["MEAN"])
            for c in range(4):
                fi, sq = ln_state["sq"][c]
                mm(P, K.ps[7][:], K.ones32[:], sq[:], c == 0, c == 3, ["ones32", ("F", fi)], [("ps", 7)])

        def ln_c1():
            fi, msq = Fnew()
            P.dve(_tt(msq[:], MEAN[:], MEAN[:], ALU.mult), reads=["MEAN"], writes=[("F", fi)])
            P.dve(_stt(RLN[:], K.ps[7][:], 1.0 / 512, msq[:], ALU.mult, ALU.subtract), reads=[("ps", 7), ("F", fi)],
                  writes=["RLN"])
            P.act(_act(RLN[:], RLN[:], AF.Sqrt, bias=K.epsb[:, 0:1]), reads=["RLN", "epsb"], writes=["RLN"])
            P.dve(_recip(RLN[:], RLN[:]), reads=["RLN"], writes=["RLN"])
            for c in range(4):
                P.dve(_tt(cv[:, c, :], cv[:, c, :], MEAN[:], ALU.subtract), reads=[("cv", c), "MEAN"],
                      writes=[("cv", c)])
                P.dve(_tt(cv[:, c, :], cv[:, c, :], RLN[:], ALU.mult), reads=[("cv", c), "RLN"], writes=[("cv", c)])

        def ln_c2():
            tts = []
            for c in range(4):
                ft, tt_ = Fnew()
                P.act(_act(tt_[:], cv[:, c, :], AF.Tanh, bias=lnh[:, 4 + c:5 + c], scale=lnh[:, c:c + 1]),
                      reads=[("cv", c), "lnh"], writes=[("F", ft)])
                tts.append((ft, tt_))
            for c in range(4):
                ft, tt_ = tts[c]
                P.dve(_ts(cv[:, c, :], cv[:, c, :], lnh[:, c:c + 1], lnh[:, 4 + c:5 + c], ALU.mult, ALU.add),
                      reads=[("cv", c), "lnh"], writes=[("cv", c)])
                P.dve(_stt(UT[:, c, :], tt_[:], 1.0, cv[:, c, :], ALU.add, ALU.mult), reads=[("F", ft), ("cv", c)],
                      writes=[("UT", c)])

        pending_fin = [None]
        for h in range(8):
            bo = 4 + (h % 2)
            blocks = []
            for jp in range(j):
                ci = h * j + jp
                cb = (kv_base + ci) % 4
                for b4 in range(4):
                    blocks.append((KTc[cb][0:96, b4 * 128:(b4 + 1) * 128], Vc[cb][:, b4, :], 0, False,
                                   [("KTc", cb)], [("Vc", cb)], ci if b4 == 0 else None))
            for b in range(4):
                blocks.append((KTs[0:96, h, b * 128:(b + 1) * 128], Vs[:, h, b, :], b * 128, True,
                               [("KTn", h), ("KTr", h)], vs_tok, None))
            nb = len(blocks)
            LOOK = 2
            pts = {}

            def emit_S(bi, h=h, blocks=blocks, pts=pts):
                kT, vv, q0, diag, ktok, vtok, ci0 = blocks[bi]
                if ci0 is not None:
                    ensure(ci0 + 3)
                bs = K.rot("pA", [0, 1, 2, 3])
                mm(P, K.ps[bs][:, q0:T], kT, QT[0:96, h, q0:T], True, True,
                   ktok + [("QTn", h), ("QTr", h)], [("ps", bs)])
                pi = K.rot("PT", [0, 1, 2, 3])
                pts[bi] = pi
                P.act(_act(PT[pi][:, q0:T], K.ps[bs][:, q0:T], AF.Exp, scale=ATT_SCALE), reads=[("ps", bs)],
                      writes=[("PT", pi)])
                if diag:
                    P.dve(_tt(PT[pi][:, q0:q0 + 128], PT[pi][:, q0:q0 + 128], mask[:], ALU.mult),
                          reads=[("PT", pi), "mask"], writes=[("PT", pi)])

            def emit_PV(bi, h=h, blocks=blocks, pts=pts, bo=bo, nb=nb):
                kT, vv, q0, diag, ktok, vtok, ci0 = blocks[bi]
                pi = pts[bi]
                mm(P, K.ps[bo][0:65, q0:T], vv, PT[pi][:, q0:T], bi == 0, bi == nb - 1,
                   vtok + [("PT", pi)], [("ps", bo)])

            for bi in range(nb + LOOK):
                if bi < nb:
                    emit_S(bi)
                if bi == LOOK and pending_fin[0] is not None:
                    pending_fin[0]()
                    pending_fin[0] = None
                if bi >= LOOK:
                    emit_PV(bi - LOOK)
            P.dve(_recip(RD[64:65, :], K.ps[bo][64:65, :]), reads=[("ps", bo)], writes=["RD"])

            def fin(h=h, bo=bo):
                bb = 6
                mm(P, K.ps[bb][0:64, :], K.ones32[64:65, 0:64], RD[64:65, :], True, True, ["ones32", "RD"],
                   [("ps", bb)])
                fi, bcs = Fnew()
                P.dve(_copy(bcs[0:64, :], K.ps[bb][0:64, :]), reads=[("ps", bb)], writes=[("F", fi)])
                hp = (h % 2) * 64
                P.dve(_tt(oT[hp:hp + 64, h // 2, :], K.ps[bo][0:64, :], bcs[0:64, :], ALU.mult),
                      reads=[("ps", bo), ("F", fi)], writes=[HOB])

            pending_fin[0] = fin
            if h < 4:
                conv_chunk(h)
            if j + 1 < NT:
                if h == 1:
                    rope_dve(j + 1)
                elif h == 2:
                    rope_act()
            if h == 3:
                ln_a()
            elif h == 4:
                ln_b()
            elif h == 5:
                ln_c1()
            elif h == 6:
                ln_c2()
        pending_fin[0]()
        P.pool(_copy(up[:, :, 0:30], up[:, :, T:T + 30]), reads=[("u", c) for c in range(4)],
               writes=[("u", c) for c in range(4)])
        kv_base += len(chunks)
        ut_tok = [("UT", c) for c in range(4)]
        for s in range(4):
            g = j * 4 + s
            b = K.rot("xs", [0, 1, 2])
            P.dma("sp", xs[b][:], xrow(xin, g), reads=[("xd", xin_name, g)], writes=[("xs", b)])
            for half in range(2):
                by = K.rot("pY", [0, 1, 2, 3])
                hs = slice(half * 512, (half + 1) * 512)
                for c in range(4):
                    mm(P, K.ps[by][:], UT[:, c, s * 128:(s + 1) * 128], w_oc[:, c, hs], c == 0, False,
                       ut_tok + ["w_oc%d" % c], [("ps", by)])
                for h2 in range(4):
                    mm(P, K.ps[by][:], oT[:, h2, s * 128:(s + 1) * 128], w_oh[:, h2, hs], False, h2 == 3,
                       [HOB, "w_oh%d" % h2], [("ps", by)])
                P.dve(_tt(xs[b][:, hs], xs[b][:, hs], K.ps[by][:], ALU.add), reads=[("ps", by), ("xs", b)],
                      writes=[("xs", b)])
            P.act(_sq_accum(junk, xs[b], K.ss_all, g), reads=[("xs", b)], writes=[("ss", g), "junk"])
            P.dma("pool", xrow(xout, g), xs[b][:], reads=[("xs", b)], writes=[("xd", xout_name, g)])


def pack_weights(inp):
    f = lambda a: np.ascontiguousarray(np.asarray(a, dtype=np.float32))
    W = {}
    W["w_up"] = f(inp["w_up"])
    W["w_down"] = f(inp["w_down"])
    fw = np.asarray(inp["ffn_conv_w"], np.float32)
    fb = np.asarray(inp["ffn_conv_b"], np.float32)
    fv = np.concatenate([fw, fb[:, None, :]], axis=1)
    fv = fv.reshape(2, 4, NF, 128).transpose(0, 3, 2, 1)
    W["ffn_vec"] = f(fv.reshape(2, 128, NF * 4))
    W["norm_ffn"] = f(np.asarray(inp["norm_ffn"]).reshape(2, 1, D))
    W["final_norm"] = f(np.asarray(inp["final_norm"]).reshape(1, D))
    W["norm_mix_o"] = f(np.asarray(inp["norm_mix_o"]).reshape(1, D))
    W["pool_scale"] = f(np.asarray(inp["pool_scale"]).reshape(1, D))
    W["pool_w"] = f(np.asarray(inp["pool_w"])[0])
    w_in = np.asarray(inp["w_in"], np.float32)[0]
    kr = w_in[:, 1664:1696]
    krs = np.concatenate([kr[:, 16:32], kr[:, 0:16]], axis=1)
    W["w_in_ext"] = f(np.concatenate([w_in[:, 0:1664], kr, kr, kr, krs, krs, krs], axis=1))
    w_uq = np.asarray(inp["w_uq"], np.float32)[0].reshape(384, 8, 96)
    qs = np.concatenate([w_uq[:, :, 0:64], w_uq[:, :, 80:96], w_uq[:, :, 64:80]], axis=2)
    W["w_uq_ext"] = f(np.concatenate([w_uq, qs], axis=2).reshape(384, 8 * 192))
    w_ukv = np.asarray(inp["w_ukv"], np.float32)[0].reshape(256, 8, 128)
    W["w_kn"] = f(w_ukv[:, :, 0:64].reshape(256, 512))
    W["w_v"] = f(w_ukv[:, :, 64:128].reshape(256, 512))
    W["w_out"] = f(np.asarray(inp["w_out"])[0])
    W["norm_mix_e"] = f(np.asarray(inp["norm_mix_e"]).reshape(1, D))
    mv = np.zeros((128, MV_COLS), np.float32)
    cw = np.asarray(inp["conv_w"], np.float32)[0]
    mv[:, MV_CW:MV_CW + 124] = cw.T.reshape(4, 128, 31).transpose(1, 0, 2).reshape(128, 124)
    pc = lambda v, n: np.asarray(v, np.float32).reshape(n, 128).T
    mv[:, MV_CB:MV_CB + 4] = pc(inp["conv_b"][0], 4)
    mv[:, MV_LG:MV_LG + 4] = pc(inp["conv_ln_g"][0], 4)
    mv[:, MV_LB:MV_LB + 4] = pc(inp["conv_ln_b"][0], 4)
    mv[:, MV_GQ:MV_GQ + 3] = pc(inp["q_norm_g"][0], 3)
    mv[:, MV_GKV:MV_GKV + 2] = pc(inp["kv_norm_g"][0], 2)
    invf = (np.float32(1.0) / (np.float32(10000.0) ** (np.arange(0, 32, 2, dtype=np.float32) / np.float32(32))))
    mv[64:96, MV_INVF] = np.concatenate([invf, invf]).astype(np.float32)
    mv[64:80, MV_SGN] = -1.0
    mv[80:96, MV_SGN] = 1.0
    W["mix_vec"] = f(mv)
    return W


def run_phases(phases, xs_list, W, ss_list=None, n_cores=8, pos=None):
    need_stats = ss_list is None
    nc, K = build(phases, need_stats)
    in_maps = []
    for c in range(n_cores):
        m = {}
        for name in K.din:
            if name == "xin":
                m[name] = xs_list[c]
            elif name == "ss_in":
                m[name] = ss_list[c]
            elif name == "pos":
                m[name] = pos[c]
            else:
                w = W[name]
                if K.din[name][1]:
                    padrow = np.full(w.shape[:-2] + (1, w.shape[-1]), float(c), w.dtype)
                    w = np.concatenate([w, padrow], axis=-2)
                m[name] = w
        in_maps.append(m)
    res = run_bass_kernel_spmd(nc, in_maps, core_ids=list(range(n_cores)))
    return [r["xout"] for r in res.results], [r["ss_out"] for r in res.results]


FUSED = True


def kernel(**inputs):
    n = 8
    W = pack_weights(inputs)
    x = np.asarray(inputs["x"], np.float32)
    pos = np.asarray(inputs["positions"], np.int32)
    xs_list = [np.ascontiguousarray(x[c]) for c in range(n)]
    pos_list = [np.ascontiguousarray(pos[c:c + 1]) for c in range(n)]
    if FUSED:
        outs, _ = run_phases(list(PHASES_ALL), xs_list, W, None, n_cores=n, pos=pos_list)
    else:
        outs, ss = run_phases(["mix0"], xs_list, W, None, n_cores=n, pos=pos_list)
        for ph in ("ffn0", "mix1", "ffn1", "final"):
            outs, ss = run_phases([ph], outs, W, ss, n_cores=n, pos=pos_list)
    return np.stack(outs, axis=0).astype(np.float32)
```

```python
import contextlib
import numpy as np
import concourse.bass as bass
import concourse.mybir as mybir
from concourse.bass_utils import run_bass_kernel_spmd

F32 = mybir.dt.float32
BF16 = mybir.dt.bfloat16
I32 = mybir.dt.int32
AF = mybir.ActivationFunctionType
ALU = mybir.AluOpType

ENGS = ("pe", "act", "dve", "pool", "sp")
N_DMA_SEMS = 6
DMA_INFLIGHT = {"sp": 6, "pool": 4, "act": 2, "pe": 2, "dve": 2}

S = 4096
D = 1024
T = 512
NT = S // T
NSUB = S // 128
DFF = 2816
NF = DFF // 128
EPS = 1e-6
TWO_PI = float(2 * np.pi)
C1 = 6.28125
C2 = float(2 * np.pi - 6.28125)
ATT_SCALE = float(1.0 / np.sqrt(96.0))


class Op:
    __slots__ = ("eng", "fn", "deps", "sig", "needs_sig", "is_dma", "dsem", "dval")

    def __init__(self, eng, fn, is_dma):
        self.eng = eng
        self.fn = fn
        self.deps = ()
        self.sig = 0
        self.needs_sig = False
        self.is_dma = is_dma
        self.dsem = None
        self.dval = 0


class Prog:
    def __init__(self, nc):
        self.nc = nc
        self.ops = {e: [] for e in ENGS}
        self.last_w = {}
        self.readers = {}
        self.dma_count = {e: 0 for e in ENGS}
        self.all_ops = []

    def add(self, eng, fn, reads=(), writes=(), dma=False, deps=()):
        op = Op(eng, fn, dma)
        dset = set(d for d in deps if d is not None)
        lw = self.last_w
        rd = self.readers
        for t in reads:
            w = lw.get(t)
            if w is not None:
                dset.add(w)
        for t in writes:
            w = lw.get(t)
            if w is not None:
                dset.add(w)
            r = rd.get(t)
            if r:
                dset.update(r)
        for t in reads:
            rd.setdefault(t, []).append(op)
        for t in writes:
            lw[t] = op
            rd[t] = []
        dset.discard(op)
        op.deps = dset
        if dma:
            k = self.dma_count[eng]
            self.dma_count[eng] = k + 1
            nfl = DMA_INFLIGHT[eng]
            op.dsem = (eng, k % nfl)
            op.dval = 16 * (k // nfl + 1)
        self.ops[eng].append(op)
        self.all_ops.append(op)
        return op

    def pe(self, fn, reads=(), writes=()):
        return self.add("pe", fn, reads, writes)

    def act(self, fn, reads=(), writes=()):
        return self.add("act", fn, reads, writes)

    def dve(self, fn, reads=(), writes=()):
        return self.add("dve", fn, reads, writes)

    def pool(self, fn, reads=(), writes=()):
        return self.add("pool", fn, reads, writes)

    def dma(self, eng, out, in_, reads=(), writes=()):
        return self.add(eng, lambda e: e.dma_start(out=out, in_=in_), reads, writes, dma=True)

    def barrier(self):
        lasts = []
        for e in ENGS:
            if self.ops[e]:
                lasts.append(self.ops[e][-1])
            n = 0
            for op in reversed(self.ops[e]):
                if op.is_dma:
                    lasts.append(op)
                    n += 1
                    if n >= N_DMA_SEMS:
                        break
        for e in ENGS:
            self.add(e, None, deps=lasts)
        self.last_w = {}
        self.readers = {}

    def finish(self, final_deps):
        self.add("sp", None, deps=final_deps)

    def emit(self):
        nc = self.nc
        for op in self.all_ops:
            for d in op.deps:
                if d.is_dma:
                    continue
                if d.eng == "pe" and op.eng == "pe":
                    continue
                d.needs_sig = True
        for e in ENGS:
            c = 0
            for op in self.ops[e]:
                if op.needs_sig and not op.is_dma:
                    c += 1
                    op.sig = c
        with contextlib.ExitStack() as st:
            esem = {e: st.enter_context(nc.semaphore("s_" + e)) for e in ENGS}
            dsem = {}
            for e in ENGS:
                if self.dma_count[e]:
                    for i in range(N_DMA_SEMS):
                        dsem[(e, i)] = st.enter_context(nc.semaphore("d_%s%d" % (e, i)))
            block = st.enter_context(nc.Block())

            def run(ename, eng):
                seen = {}
                for op in self.ops[ename]:
                    need = {}
                    for d in op.deps:
                        if d.is_dma:
                            key = ("d", d.dsem)
                            v = d.dval
                        else:
                            if d.eng == "pe" and ename == "pe":
                                continue
                            key = ("e", d.eng)
                            v = d.sig
                        if need.get(key, 0) < v:
                            need[key] = v
                    if op.is_dma and op.dval > 16:
                        key = ("d", op.dsem)
                        v = op.dval - 16
                        if need.get(key, 0) < v:
                            need[key] = v
                    for key, v in need.items():
                        if seen.get(key, 0) >= v:
                            continue
                        seen[key] = v
                        sem = dsem[key[1]] if key[0] == "d" else esem[key[1]]
                        eng.wait_ge(sem, v)
                    if op.fn is None:
                        if op.needs_sig:
                            eng.nop().then_inc(esem[ename], 1)
                        continue
                    ins = op.fn(eng)
                    if op.is_dma:
                        ins.then_inc(dsem[op.dsem], 16)
                    elif op.needs_sig:
                        ins.then_inc(esem[ename], 1)

            @block.tensor
            def _(eng):
                run("pe", eng)

            @block.scalar
            def _(eng):
                run("act", eng)

            @block.vector
            def _(eng):
                run("dve", eng)

            @block.gpsimd
            def _(eng):
                run("pool", eng)

            @block.sync
            def _(eng):
                run("sp", eng)


class Ctx:
    SB_BASE = 16384 + 512

    def __init__(self, nc):
        self.nc = nc
        self.P = Prog(nc)
        self.ps = [nc.alloc_psum_tensor("ps%d" % i, [128, 512], F32) for i in range(8)]
        self.off = Ctx.SB_BASE
        self.uid = 0
        self.rot_state = {}
        self.din = {}

    def sb(self, shape, dt, name=None):
        self.uid += 1
        esz = 4 if dt in (F32, I32) else 2
        n = 1
        for s in shape[1:]:
            n *= s
        nbytes = (n * esz + 63) // 64 * 64
        t = self.nc.alloc_sbuf_tensor_at("%s_%d" % (name or "t", self.uid), list(shape), dt, offset=self.off)
        self.last_off = self.off
        self.off += nbytes
        if self.off > 16384 + 212000:
            raise RuntimeError("SBUF overflow: %d" % self.off)
        return t

    def sb_at(self, shape, dt, name, offset):
        self.uid += 1
        return self.nc.alloc_sbuf_tensor_at("%s_%d" % (name, self.uid), list(shape), dt, offset=offset)

    def rot(self, name, lst):
        i = self.rot_state.get(name, 0)
        self.rot_state[name] = i + 1
        return lst[i % len(lst)]

    def inp(self, name, shape, dt=F32):
        shape = list(shape)
        nbytes = 4 * int(np.prod(shape))
        pad = nbytes >= (1 << 20) and name not in ("xin",)
        if name not in self.din:
            dshape = list(shape)
            if pad:
                dshape[-2] += 1
            self.din[name] = (self.nc.dram_tensor(name, dshape, dt, kind="ExternalInput").ap(), pad, shape[-2])
        ap, pad, n = self.din[name]
        if pad:
            ap = ap[:, 0:n, :] if len(shape) == 3 else ap[0:n, :]
        return ap


def mm(P, out, lhsT, rhs, start, stop, reads, writes):
    return P.pe(lambda e: e.matmul(out, lhsT, rhs, start=start, stop=stop), reads, writes)


def setup_common(K):
    P = K.P
    K.ss_all = K.sb([128, NSUB], F32, "ss_all")
    K.rstd_all = K.sb([128, NSUB], F32, "rstd_all")
    K.epsb = K.sb([128, 1], F32, "epsb")
    K.ident_bf = K.sb([128, 128], BF16, "identbf")
    K.ident32 = K.sb([128, 128], F32, "ident32")
    K.ones32 = K.sb([128, 128], F32, "ones32")
    K.persist_end = K.off
    P.pool(lambda e: e.memset(K.epsb[:], EPS), writes=["epsb"])
    P.pool(lambda e: e.memset(K.ones32[:], 1.0), writes=["ones32"])
    P.pool(lambda e: e.memset(K.ident32[:], 1.0), writes=["id32"])
    P.pool(lambda e: e.affine_select(K.ident32[:], K.ident32[:], [[-1, 128]], ALU.is_equal, 0.0,
                                     base=0, channel_multiplier=1), reads=["id32"], writes=["id32"])
    P.dve(lambda e: e.tensor_copy(K.ident_bf[:], K.ident32[:]), reads=["id32"], writes=["idbf"])


def xrow(xd, g):
    return xd[g * 128:(g + 1) * 128, :]


def stats_prologue(K, xin, xin_name):
    P = K.P
    NB = 8
    xs = [K.sb([128, D], F32, "pxs") for _ in range(NB)]
    junk = K.sb([128, D], BF16, "pjunk")
    for g in range(NSUB):
        b = g % NB
        P.dma("sp", xs[b][:], xrow(xin, g), reads=[("xd", xin_name, g)], writes=[("pxs", b)])
        P.act(_sq_accum(junk, xs[b], K.ss_all, g), reads=[("pxs", b)], writes=[("ss", g), "pjunk"])


def _sq_accum(junk, src, ss, g):
    return lambda e: e.activation(junk[:], src[:], AF.Square, accum_out=ss[:, g:g + 1])


def rstd_from_ss(K):
    P = K.P
    rd = [("ss", g) for g in range(NSUB)]
    P.act(lambda e: e.activation(K.rstd_all[:], K.ss_all[:], AF.Sqrt, bias=K.epsb[:, 0:1], scale=1.0 / D),
          reads=rd + ["epsb"], writes=["rstd_tmp"])
    P.dve(lambda e: e.reciprocal(K.rstd_all[:], K.rstd_all[:]), reads=["rstd_tmp"], writes=["rstd_all"])


def load_bcast(K, dst, src_row_ap, tok):
    n = dst.shape[-1]
    K.P.dma("sp", dst[:], src_row_ap.broadcast_to([128, n]), writes=[tok])


def make_h(K, xin, xin_name, g, xs, hb, gbc, gtok, out_dt_tag):
    P = K.P
    b = K.rot("xs", [0, 1, 2])
    P.dma("sp", xs[b][:], xrow(xin, g), reads=[("xd", xin_name, g)], writes=[("xs", b)])
    hbuf = K.rot("hb" + out_dt_tag, [0, 1])
    P.dve(lambda e: e.scalar_tensor_tensor(hb[hbuf][:], xs[b][:], K.rstd_all[:, g:g + 1], gbc[:],
                                           ALU.mult, ALU.mult),
          reads=[("xs", b), "rstd_all", gtok], writes=[("hb" + out_dt_tag, hbuf)])
    return hbuf


def transpose_bf(K, hb, hbuf, hT, s, bank):
    P = K.P
    pst = K.ps[bank][:].bitcast(BF16)
    for c in range(8):
        P.pe(_tr(pst[:, c * 128:(c + 1) * 128], hb[hbuf][:, c * 128:(c + 1) * 128], K.ident_bf),
             reads=[("hbb", hbuf), "idbf"], writes=[("ps", bank)])
    src = pst.rearrange("p (c t) -> p c t", c=8)
    dst = hT[:, :, s * 128:(s + 1) * 128]
    P.act(lambda e: e.activation(dst, src, AF.Copy), reads=[("ps", bank)], writes=[("hT", s)])


def _tr(out, in_, ident):
    return lambda e: e.transpose(out, in_, ident[:])


NPC = 6


def ffn_weight_pieces(K, l, wup, wdn):
    P = K.P
    w_up_v = K.inp("w_up", [2, D, 2 * DFF])[l].rearrange("(k p) n -> p k n", p=128)
    w_dn_v = K.inp("w_down", [2, DFF, D])[l].rearrange("(f p) n -> p f n", p=128)
    out = []
    for i in range(NPC):
        c0 = i * 512
        c1 = min(DFF, c0 + 512)
        for base in (0, DFF):
            for kh in range(2):
                ks = slice(kh * 4, kh * 4 + 4)
                out.append(lambda base=base, c0=c0, c1=c1, i=i, kh=kh, ks=ks: P.dma(
                    "pool", wup[:, ks, base + c0:base + c1], w_up_v[:, ks, base + c0:base + c1],
                    writes=[("wup", base, i, kh)]))
    for f2 in range(NF // 2):
        out.append(lambda f2=f2: P.dma("pool", wdn[:, 2 * f2:2 * f2 + 2, :], w_dn_v[:, 2 * f2:2 * f2 + 2, :],
                                       writes=[("wdn", f2)]))
    return out


def ffn_alloc_weights(K):
    wup = K.sb([128, 8, 2 * DFF], BF16, "wup")
    wdn = K.sb([128, NF, D], BF16, "wdn")
    return wup, wdn


def phase_ffn(K, l, xin, xin_name, xout, xout_name, pre=None, fuse_final=False):
    nc, P = K.nc, K.P
    K.off = K.persist_end
    K.rot_state = {}
    fvec_d = K.inp("ffn_vec", [2, 128, NF * 4])[l]
    gain_d = K.inp("norm_ffn", [2, 1, D])[l]
    if pre is None:
        wup, wdn = ffn_alloc_weights(K)
    else:
        wup, wdn = K.prefetched
        K.off = K.prefetched_end
    gbc = K.sb([128, D], F32, "gbc")
    fvec = K.sb([128, NF, 4], F32, "fvec")
    halo = K.sb([128, NF, 2], F32, "halo")
    xs = [K.sb([128, D], F32, "xs") for _ in range(3)]
    hb = [K.sb([128, D], BF16, "hb") for _ in range(2)]
    hT = K.sb([128, 8, T], BF16, "hT")
    gT = K.sb([128, NF, T], BF16, "gT")
    ub = [K.sb([128, T + 2], F32, "ub") for _ in range(2)]
    acc = [K.sb([128, T], F32, "acc") for _ in range(2)]
    tt = [K.sb([128, T], F32, "tt") for _ in range(2)]
    junk = K.sb([128, D], BF16, "junk")
    if fuse_final:
        gfin = K.sb([128, D], F32, "gfin")
        rsf = K.sb([128, 4], F32, "rsf")
        load_bcast(K, gfin, K.inp("final_norm", [1, D]), "gfin")

    load_bcast(K, gbc, gain_d, "gbc")
    P.dma("sp", fvec[:].rearrange("p f k -> p (f k)"), fvec_d, writes=["fvec"])
    P.pool(lambda e: e.memset(halo[:], 0.0), writes=[("halo", f) for f in range(NF)])
    pend = []
    if pre is None:
        pend = ffn_weight_pieces(K, l, wup, wdn)
        for issue in pend[:8]:
            issue()
        pend = pend[8:]
    rstd_from_ss(K)

    def norm_tile(jn, subs=(0, 1, 2, 3), pre=None):
        for s in subs:
            g = jn * 4 + s
            if pre is not None and s in pre:
                hbuf = pre[s]
            else:
                hbuf = make_h(K, xin, xin_name, g, xs, hb, gbc, "gbc", "b")
            transpose_bf(K, hb, hbuf, hT, s, K.rot("pst", [0, 1]))

    finals = []
    norm_tile(0)
    for j in range(NT):
        hT_tok = [("hT", s) for s in range(4)]
        for f in range(NF):
            for _ in range(1 if f < 16 else 2):
                if pend:
                    pend.pop(0)()
            if f == NF - 2 and j + 1 < NT:
                pre_h = {s_: make_h(K, xin, xin_name, (j + 1) * 4 + s_, xs, hb, gbc, "gbc", "b") for s_ in (0, 1)}
            bu = K.rot("pu", [2, 3])
            bv = K.rot("pv", [4, 5])
            pi = f // 4
            for k in range(8):
                mm(P, K.ps[bu][:], wup[:, k, f * 128:(f + 1) * 128], hT[:, k, :], k == 0, k == 7,
                   hT_tok + [("wup", 0, pi, k // 4)], [("ps", bu)])
            for k in range(8):
                mm(P, K.ps[bv][:], wup[:, k, DFF + f * 128:DFF + (f + 1) * 128], hT[:, k, :], k == 0, k == 7,
                   hT_tok + [("wup", DFF, pi, k // 4)], [("ps", bv)])
            ui = K.rot("ub", [0, 1])
            u = ub[ui]
            a = acc[ui]
            t = tt[ui]
            P.pool(_copy(u[:, 0:2], halo[:, f, :]), reads=[("halo", f)], writes=[("ubh", ui)])
            P.act(_actcopy(u[:, 2:T + 2], K.ps[bu][:]), reads=[("ps", bu)], writes=[("ub", ui)])
            P.pool(_copy(halo[:, f, :], u[:, T:T + 2]), reads=[("ub", ui)], writes=[("halo", f)])
            P.act(_act(a[:], K.ps[bu][:], AF.Identity, bias=fvec[:, f, 3:4], scale=fvec[:, f, 2:3]),
                  reads=[("ps", bu), "fvec"], writes=[("acc", ui)])
            P.dve(_stt(a[:], u[:, 0:T], fvec[:, f, 0:1], a[:], ALU.mult, ALU.add),
                  reads=[("ubh", ui), ("ub", ui), "fvec", ("acc", ui)], writes=[("acc", ui)])
            P.dve(_stt(a[:], u[:, 1:T + 1], fvec[:, f, 1:2], a[:], ALU.mult, ALU.add),
                  reads=[("ubh", ui), ("ub", ui), ("acc", ui)], writes=[("acc", ui)])
            P.act(_tanh_half(t[:], a[:]), reads=[("acc", ui)], writes=[("tt", ui)])
            P.dve(_stt(t[:], t[:], 1.0, a[:], ALU.add, ALU.mult), reads=[("tt", ui), ("acc", ui)],
                  writes=[("tt", ui)])
            P.dve(_stt(gT[:, f, :], t[:], 0.5, K.ps[bv][:], ALU.mult, ALU.mult),
                  reads=[("tt", ui), ("ps", bv)], writes=[("gT", f)])
        while pend:
            pend.pop(0)()
        gT_tok = [("gT", f) for f in range(NF)]
        for s in range(4):
            if s == 0 and j + 1 < NT:
                norm_tile(j + 1, pre=pre_h)
            g = j * 4 + s
            b = K.rot("xs", [0, 1, 2])
            P.dma("sp", xs[b][:], xrow(xin, g), reads=[("xd", xin_name, g)], writes=[("xs", b)])
            for half in range(2):
                by = K.rot("py", [6, 7])
                for f in range(NF):
                    mm(P, K.ps[by][:], gT[:, f, s * 128:(s + 1) * 128], wdn[:, f, half * 512:(half + 1) * 512],
                       f == 0, f == NF - 1, [("gT", f), ("wdn", f // 2)], [("ps", by)])
                P.dve(_tt(xs[b][:, half * 512:(half + 1) * 512], xs[b][:, half * 512:(half + 1) * 512],
                          K.ps[by][:], ALU.add), reads=[("ps", by), ("xs", b)], writes=[("xs", b)])
            P.act(_sq_accum(junk, xs[b], K.ss_all, g), reads=[("xs", b)], writes=[("ss", g), "junk"])
            if fuse_final:
                ri = s
                P.act(_act(rsf[:, ri:ri + 1], K.ss_all[:, g:g + 1], AF.Sqrt, bias=K.epsb[:, 0:1], scale=1.0 / D),
                      reads=[("ss", g), "epsb"], writes=[("rsf", ri)])
                P.dve(_recip(rsf[:, ri:ri + 1], rsf[:, ri:ri + 1]), reads=[("rsf", ri)], writes=[("rsf", ri)])
                P.dve(_stt(xs[b][:], xs[b][:], rsf[:, ri:ri + 1], gfin[:], ALU.mult, ALU.mult),
                      reads=[("xs", b), ("rsf", ri), "gfin"], writes=[("xs", b)])
            finals.append(P.dma("pool", xrow(xout, g), xs[b][:], reads=[("xs", b)], writes=[("xd", xout_name, g)]))
    return finals


def _copy(out, in_):
    return lambda e: e.tensor_copy(out, in_)


def _actcopy(out, in_):
    return lambda e: e.activation(out, in_, AF.Copy)


def _ts2(out, in0, s1, s2):
    return lambda e: e.tensor_scalar(out, in0, s1, s2, ALU.mult, ALU.add)


def _stt(out, in0, sc, in1, op0, op1):
    return lambda e: e.scalar_tensor_tensor(out, in0, sc, in1, op0, op1)


def _tt(out, in0, in1, op):
    return lambda e: e.tensor_tensor(out, in0, in1, op)


def _tanh_half(out, in_):
    return lambda e: e.activation(out, in_, AF.Tanh, scale=0.5)


def phase_final(K, xin, xin_name, xout, xout_name):
    P = K.P
    K.off = K.persist_end
    K.rot_state = {}
    gain_d = K.inp("final_norm", [1, D])
    gbc = K.sb([128, D], F32, "gbc")
    xs = [K.sb([128, D], F32, "xs") for _ in range(3)]
    ob = [K.sb([128, D], F32, "ob") for _ in range(2)]
    load_bcast(K, gbc, gain_d, "gbc")
    rstd_from_ss(K)
    finals = []
    for g in range(NSUB):
        b = K.rot("xs", [0, 1, 2])
        o = K.rot("ob", [0, 1])
        P.dma("sp", xs[b][:], xrow(xin, g), reads=[("xd", xin_name, g)], writes=[("xs", b)])
        P.dve(_stt(ob[o][:], xs[b][:], K.rstd_all[:, g:g + 1], gbc[:], ALU.mult, ALU.mult),
              reads=[("xs", b), "rstd_all", "gbc"], writes=[("ob", o)])
        finals.append(P.dma("pool", xrow(xout, g), ob[o][:], reads=[("ob", o)], writes=[("xd", xout_name, g)]))
    return finals


def phase_mix1(K, xin, xin_name, xout, xout_name, prefetch_ffn=None):
    P = K.P
    K.off = K.persist_end
    K.rot_state = {}
    HL = 16
    gain_d = K.inp("norm_mix_o", [1, D])
    psc_d = K.inp("pool_scale", [1, D])
    pw_d = K.inp("pool_w", [4, 256, 256])
    wpieces = []
    if prefetch_ffn is not None:
        wup_n, wdn_n = ffn_alloc_weights(K)
        K.prefetched = (wup_n, wdn_n)
        K.prefetched_end = K.off
        wpieces = ffn_weight_pieces(K, prefetch_ffn, wup_n, wdn_n)
    gbc = K.sb([128, D], F32, "gbc")
    pw = K.sb([128, 4, 2, 256], BF16, "pw")
    xs = [K.sb([128, D], F32, "xs") for _ in range(3)]
    hb = [K.sb([128, D], F32, "hb32") for _ in range(2)]
    hT = K.sb([128, 8, HL + T], F32, "hT32")
    sA0 = K.sb([128, 2, HL + T], F32, "sA0")
    sa0_off = K.last_off
    sB0 = K.sb([128, 2, HL + T], F32, "sB0")
    sA1 = K.sb([128, 2, HL + T], F32, "sA1")
    sa1_off = K.last_off
    sB1 = K.sb([128, 2, HL + T], F32, "sB1")
    pw32 = K.sb_at([128, 4, 2, 256], F32, "pw32", sa0_off)
    psc = K.sb_at([128, D], F32, "psc", sa1_off)
    pT = K.sb([128, 8, T], BF16, "pT")
    ic = K.sb([128, 4, 16], F32, "ic")
    ici = K.sb([128, 16], F32, "ici")
    junk = K.sb([128, D], BF16, "junk")
    W = (2, 4, 8, 16)

    load_bcast(K, gbc, gain_d, "gbc")
    load_bcast(K, psc, psc_d, "psc")
    for gi in range(4):
        P.dma("sp", pw32[:, gi, :, :], pw_d[gi].rearrange("(k p) n -> p k n", p=128), writes=[("pw32", gi)])
    for gi in range(4):
        for kc in range(2):
            P.dve(_tt(pw[:, gi, kc, :], pw32[:, gi, kc, :], psc[:, gi * 256:(gi + 1) * 256], ALU.mult),
                  reads=[("pw32", gi), "psc"], writes=[("pw", gi), "sA0", "sB0", "sA1"])
    P.pool(lambda e: e.iota(ici[:], [[1, 16]], base=1, channel_multiplier=0, allow_small_or_imprecise_dtypes=True), writes=["ici"])
    for gi in range(4):
        P.dve(_tsmin(ic[:, gi, :], ici[:], float(W[gi])), reads=["ici"], writes=[("ic", gi)])
        P.dve(_recip(ic[:, gi, :], ic[:, gi, :]), reads=[("ic", gi)], writes=[("ic", gi)])
    P.pool(lambda e: e.memset(hT[:, :, 0:HL], 0.0), writes=[("hTh", c) for c in range(8)])
    rstd_from_ss(K)

    def tileA(j):
            for s in range(4):
                g = j * 4 + s
                hbuf = make_h(K, xin, xin_name, g, xs, hb, gbc, "gbc", "f")
                for hf in range(2):
                    bank = K.rot("pst", [0, 1, 2, 3])
                    for c4 in range(4):
                        c = hf * 4 + c4
                        P.pe(_tr(K.ps[bank][:, c4 * 128:(c4 + 1) * 128], hb[hbuf][:, c * 128:(c + 1) * 128], K.ident32),
                             reads=[("hbf", hbuf), "id32"], writes=[("ps", bank)])
                    src = K.ps[bank][:].rearrange("p (c t) -> p c t", c=4)
                    dst = hT[:, hf * 4:(hf + 1) * 4, HL + s * 128:HL + (s + 1) * 128]
                    P.act(_actcopy(dst, src), reads=[("ps", bank)],
                          writes=[("hT", hf * 4 + c4, s) for c4 in range(4)])

    def tileB(j):
            for gi in range(4):
                cs = slice(2 * gi, 2 * gi + 2)
                hdeps = [("hT", c, s) for c in (2 * gi, 2 * gi + 1) for s in range(4)] + \
                        [("hTh", c) for c in (2 * gi, 2 * gi + 1)]
                w = W[gi]
                L = HL + T
                weng = "dve"
                sA, sB = (sA0, sB0) if gi < 2 else (sA1, sB1)
                tA, tB = ("sA0", "sB0") if gi < 2 else ("sA1", "sB1")
                P.add(weng, _tt(sA[:, :, 1:L], hT[:, cs, 1:L], hT[:, cs, 0:L - 1], ALU.add),
                      reads=hdeps, writes=[tA])
                cur, curtok, other, othertok = sA, tA, sB, tB
                sh = 2
                while sh < w:
                    P.add(weng, _tt(other[:, :, 2 * sh - 1:L], cur[:, :, 2 * sh - 1:L], cur[:, :, sh - 1:L - sh], ALU.add),
                          reads=[curtok], writes=[othertok])
                    cur, curtok, other, othertok = other, othertok, cur, curtok
                    sh *= 2
                ptok = [("pT", c) for c in (2 * gi, 2 * gi + 1)]
                if j == 0:
                    for c2 in range(2):
                        c = 2 * gi + c2
                        P.dve(_tt(cur[:, c2, HL:HL + 16], cur[:, c2, HL:HL + 16], ic[:, gi, :], ALU.mult),
                              reads=[curtok, ("ic", gi)], writes=[curtok])
                        P.dve(_tt(pT[:, c, 0:16], cur[:, c2, HL:HL + 16], hT[:, c, HL:HL + 16], ALU.subtract),
                              reads=[curtok] + hdeps, writes=[("pT", c)])
                    P.dve(_stt(pT[:, cs, 16:T], cur[:, :, HL + 16:L], 1.0 / w, hT[:, cs, HL + 16:L],
                               ALU.mult, ALU.subtract), reads=[curtok] + hdeps + ptok, writes=ptok)
                else:
                    P.dve(_stt(pT[:, cs, :], cur[:, :, HL:L], 1.0 / w, hT[:, cs, HL:L], ALU.mult, ALU.subtract),
                          reads=[curtok] + hdeps, writes=ptok)
                P.act(_actcopy(hT[:, cs, 0:HL], hT[:, cs, T:T + HL]), reads=hdeps,
                      writes=[("hTh", c) for c in (2 * gi, 2 * gi + 1)])

    def tileC(j):
            for s in range(4):
                g = j * 4 + s
                b = K.rot("xs", [0, 1, 2])
                P.dma("sp", xs[b][:], xrow(xin, g), reads=[("xd", xin_name, g)], writes=[("xs", b)])
                for half in range(2):
                    by = K.rot("py", [4, 5, 6, 7])
                    for g2 in range(2):
                        gi = half * 2 + g2
                        for kc in range(2):
                            c = 2 * gi + kc
                            mm(P, K.ps[by][:, g2 * 256:(g2 + 1) * 256], pT[:, c, s * 128:(s + 1) * 128],
                               pw[:, gi, kc, :], kc == 0, kc == 1, [("pT", c), ("pw", gi)], [("ps", by)])
                    hs = slice(half * 512, (half + 1) * 512)
                    P.dve(_tt(xs[b][:, hs], xs[b][:, hs], K.ps[by][:], ALU.add), reads=[("ps", by), ("xs", b)],
                          writes=[("xs", b)])
                P.act(_sq_accum(junk, xs[b], K.ss_all, g), reads=[("xs", b)], writes=[("ss", g), "junk"])
                P.dma("sp", xrow(xout, g), xs[b][:], reads=[("xs", b)], writes=[("xd", xout_name, g)])

    tileA(0)
    for j in range(NT):
        for issue in (wpieces[j * 5:(j + 1) * 5] if j < NT - 1 else wpieces[j * 5:]):
            issue()
        tileB(j)
        if j + 1 < NT:
            tileA(j + 1)
        tileC(j)


def _tsmin(out, in_, v):
    return lambda e: e.tensor_scalar(out, in_, v, None, ALU.min)


def _recip(out, in_):
    return lambda e: e.reciprocal(out, in_)


PHASES_ALL = ("mix0", "ffn0", "mix1", "ffn1f")


def build(phases, need_stats):
    nc = bass.Bass("TRN2", target_bir_lowering=False)
    K = Ctx(nc)
    P = K.P
    xin = K.inp("xin", [S, D])
    xout = nc.dram_tensor("xout", [S, D], F32, kind="ExternalOutput").ap()
    setup_common(K)
    bufs = {}
    n = len(phases)
    cur, cur_name = xin, "xin"
    if n > 1:
        scratch = nc.dram_tensor("xscr", [S, D], F32).ap()
    if need_stats:
        stats_prologue(K, xin, "xin")
    else:
        ss_d = K.inp("ss_in", [128, NSUB])
        P.dma("sp", K.ss_all[:], ss_d, writes=[("ss", g) for g in range(NSUB)])
    finals = None
    for i, ph in enumerate(phases):
        last = i == n - 1
        dst, dst_name = (xout, "xout") if last else (scratch, "xscr")
        P.barrier()
        if ph == "mix0":
            from_mix0 = phase_mix0(K, cur, cur_name, dst, dst_name)
        elif ph == "ffn0":
            phase_ffn(K, 0, cur, cur_name, dst, dst_name)
        elif ph == "mix1":
            nxt = phases[i + 1] if i + 1 < n else None
            phase_mix1(K, cur, cur_name, dst, dst_name, prefetch_ffn=1 if nxt in ("ffn1", "ffn1f") else None)
        elif ph == "ffn1":
            phase_ffn(K, 1, cur, cur_name, dst, dst_name, pre=(i > 0 and phases[i - 1] == "mix1") or None)
        elif ph == "ffn1f":
            finals = phase_ffn(K, 1, cur, cur_name, dst, dst_name, pre=(i > 0 and phases[i - 1] == "mix1") or None,
                               fuse_final=True)
        elif ph == "final":
            finals = phase_final(K, cur, cur_name, dst, dst_name)
        cur, cur_name = dst, dst_name
    ss_out = nc.dram_tensor("ss_out", [128, NSUB], F32, kind="ExternalOutput").ap()
    P.barrier()
    f2 = P.dma("sp", ss_out, K.ss_all[:])
    P.barrier()
    P.finish([f2])
    P.emit()
    return nc, K


A0, G0, CQ0, CKV0, KR0, KRS0, WIN_COLS = 0, 512, 1024, 1408, 1664, 1760, 1856
MV_CW, MV_CB, MV_LG, MV_LB, MV_GQ, MV_GKV, MV_INVF, MV_SGN, MV_COLS = 0, 124, 128, 132, 136, 139, 141, 142, 143


def _act(out, in_, func, **kw):
    return lambda e: e.activation(out, in_, func, **kw)


def _ts1(out, in0, s1, op0):
    return lambda e: e.tensor_scalar(out, in0, s1, None, op0)


def _ts(out, in0, s1, s2, op0, op1):
    return lambda e: e.tensor_scalar(out, in0, s1, s2, op0, op1)


def phase_mix0(K, xin, xin_name, xout, xout_name):
    nc, P = K.nc, K.P
    K.off = K.persist_end
    K.rot_state = {}
    w_in_d = K.inp("w_in_ext", [D, WIN_COLS])
    w_uq_d = K.inp("w_uq_ext", [384, 8 * 192])
    w_kn_d = K.inp("w_kn", [256, 512])
    w_v_d = K.inp("w_v", [256, 512])
    w_out_d = K.inp("w_out", [D, D])
    mvec_d = K.inp("mix_vec", [128, MV_COLS])
    gain_d = K.inp("norm_mix_e", [1, D])
    pos_d = K.inp("pos", [1, S], I32)
    kt_d = nc.dram_tensor("kt_d", [8, NT, 96, 512], BF16).ap()
    v_d = nc.dram_tensor("v_d", [8, NT, 128, 260], BF16).ap()
    cos_d = nc.dram_tensor("cos_d", [32, S], F32).ap()
    sin_d = nc.dram_tensor("sin_d", [32, S], F32).ap()

    w_in = K.sb([128, 8, WIN_COLS], BF16, "w_in")
    w_uq = K.sb([128, 3, 8 * 192], BF16, "w_uq")
    w_kn = K.sb([128, 2, 512], BF16, "w_kn")
    w_v = K.sb([128, 2, 512], BF16, "w_v")
    w_oc = K.sb([128, 4, D], BF16, "w_oc")
    w_oh = K.sb([128, 4, D], BF16, "w_oh")
    gbc = K.sb([128, D], F32, "gbc")
    mvec = K.sb([128, MV_COLS], F32, "mvec")
    wch = K.sb([128, 4, 31], F32, "wch")
    lnh = K.sb([128, 8], F32, "lnh")
    xs = [K.sb([128, D], F32, "xs") for _ in range(3)]
    hb = [K.sb([128, D], BF16, "hb") for _ in range(2)]
    HO = K.sb([128, 8, T], BF16, "HO")
    junk = K.sb([128, D], BF16, "junk")
    up = K.sb([128, 4, 30 + T], BF16, "up")
    DGp = K.sb([128, 124, 128], BF16, "DGp")
    cv = K.sb([128, 4, T], F32, "cv")
    cqg = K.sb([128, 3, T], BF16, "cqg")
    ckvg = K.sb([128, 2, T], BF16, "ckvg")
    Fb = [K.sb([128, T], F32, "F") for _ in range(8)]
    RQ = K.sb([128, T], F32, "RQ")
    RKV = K.sb([128, T], F32, "RKV")
    RKT = K.sb([128, 4], F32, "RKT")
    CS = K.sb([128, 2, T], F32, "CS")
    CR = K.sb([128, 2, T], F32, "CR")
    RD = Fb[7]
    QT = K.sb([128, 8, T], BF16, "QT")
    KTs = K.sb([128, 8, T], BF16, "KTs")
    Vs = K.sb([128, 8, 4, 65], BF16, "Vs")
    KTc = [K.sb([128, T], BF16, "KTc") for _ in range(4)]
    Vc = [K.sb([128, 4, 65], BF16, "Vc") for _ in range(4)]
    PT = [K.sb([128, T], BF16, "PT") for _ in range(4)]
    mask32 = Fb[7]
    mask = K.sb([128, 128], BF16, "mask")
    MEAN = K.sb([128, T], F32, "MEAN")
    RLN = K.sb([128, T], F32, "RLN")
    UT = K.sb([128, 4, T], BF16, "UT")
    hT = HO
    oT = HO

    def Fnew():
        i = K.rot("F", list(range(7)))
        return i, Fb[i]

    P.dma("sp", mvec[:], mvec_d, writes=["mvec"])
    load_bcast(K, gbc, gain_d, "gbc")
    P.pool(lambda e: e.memset(Vs[:], 1.0), writes=[("Vs", s) for s in range(4)])
    P.pool(lambda e: e.memset(up[:, :, 0:30], 0.0), writes=[("u", c) for c in range(4)])
    P.pool(lambda e: e.memset(mask32[:, 0:128], 1.0), writes=["mask32"])
    P.pool(lambda e: e.affine_select(mask32[:, 0:128], mask32[:, 0:128], [[1, 128]], ALU.is_ge, 0.0,
                                     base=0, channel_multiplier=-1), reads=["mask32"], writes=["mask32"])
    P.dve(_copy(mask[:], mask32[:, 0:128]), reads=["mask32"], writes=["mask", "RD"])
    w_in_v = w_in_d.rearrange("(k p) n -> p k n", p=128)
    for i, (c0, c1) in enumerate(((0, 512), (512, 1024), (1024, 1408), (1408, WIN_COLS))):
        for kh in range(2):
            P.dma("pool", w_in[:, kh * 4:kh * 4 + 4, c0:c1], w_in_v[:, kh * 4:kh * 4 + 4, c0:c1],
                  writes=[("w_in", i, kh)])
    w_uq_v = w_uq_d.rearrange("(k p) n -> p k n", p=128)
    for k in range(3):
        P.dma("pool", w_uq[:, k, :], w_uq_v[:, k, :], writes=[("w_uq", k)])
    P.dma("pool", w_kn[:], w_kn_d.rearrange("(k p) n -> p k n", p=128), writes=["w_kn"])
    P.dma("pool", w_v[:], w_v_d.rearrange("(k p) n -> p k n", p=128), writes=["w_v"])
    w_oc_v = w_out_d[0:512, :].rearrange("(c p) n -> p c n", p=128)
    for c in range(4):
        P.dma("pool", w_oc[:, c, :], w_oc_v[:, c, :], writes=["w_oc%d" % c])
    w_oh_v = w_out_d[512:1024, :].rearrange("(q p) n -> p q n", p=128)
    for h2 in range(4):
        P.dma("pool", w_oh[:, h2, :], w_oh_v[:, h2, :], writes=["w_oh%d" % h2])
    P.dve(_ts1(wch[:].rearrange("p c k -> p (c k)"), mvec[:, MV_CW:MV_CW + 124], 0.5, ALU.mult),
          reads=["mvec"], writes=["wch"])
    P.dve(_ts1(lnh[:], mvec[:, MV_LG:MV_LG + 8], 0.5, ALU.mult), reads=["mvec"], writes=["lnh"])
    R = slice(64, 96)
    posi = Fb[0][:].bitcast(I32)
    ki = Fb[1][:].bitcast(I32)
    posf, ang, a2, xk, kf = Fb[2], Fb[3], Fb[4], Fb[5], Fb[6]
    negpi = float(-np.pi)

    def rope_dve(ch):
        cs_ = slice(ch * T, (ch + 1) * T)
        P.dma("sp", posi[R, :], pos_d[0:1, cs_].broadcast_to([32, T]), writes=[("F", 0)])
        P.dve(_copy(posf[R, :], posi[R, :]), reads=[("F", 0)], writes=[("F", 2)])
        P.dve(_ts1(ang[R, :], posf[R, :], mvec[R, MV_INVF:MV_INVF + 1], ALU.mult), reads=[("F", 2), "mvec"],
              writes=[("F", 3)])
        for which in range(2):
            if which == 1:
                P.dve(_ts1(a2[R, :], ang[R, :], float(np.pi / 2), ALU.add), reads=[("F", 3)], writes=[("F", 4)])
                src, stok = a2, ("F", 4)
            else:
                src, stok = ang, ("F", 3)
            P.dve(_ts1(xk[R, :], src[R, :], float(1.0 / TWO_PI), ALU.mult), reads=[stok], writes=[("F", 5)])
            P.dve(_copy(ki[R, :], xk[R, :]), reads=[("F", 5)], writes=[("F", 1)])
            P.dve(_copy(kf[R, :], ki[R, :]), reads=[("F", 1)], writes=[("F", 6)])
            P.dve(_stt(xk[R, :], kf[R, :], -C1, src[R, :], ALU.mult, ALU.add), reads=[("F", 6), stok],
                  writes=[("F", 5)])
            P.dve(_stt(xk[R, :], kf[R, :], -C2, xk[R, :], ALU.mult, ALU.add), reads=[("F", 6), ("F", 5)],
                  writes=[("F", 5)])
            P.dve(_ts(CS[R, 1 - which, :], xk[R, :], negpi, -negpi, ALU.max, ALU.min), reads=[("F", 5)],
                  writes=["cs%d" % (1 - which)])

    def rope_act():
        P.act(_act(CS[R, 1, :], CS[R, 1, :], AF.Sin, scale=mvec[R, MV_SGN:MV_SGN + 1]),
              reads=["cs1", "mvec"], writes=["cs1"])
        P.act(_act(CS[R, 0, :], CS[R, 0, :], AF.Sin), reads=["cs0"], writes=["cs0"])

    def rope_tables(ch):
        rope_dve(ch)
        rope_act()

    rope_tables(0)
    rstd_from_ss(K)

    HOB = "HObuf"
    kv_base = 0
    for j in range(NT):
        ts_ = slice(j * T, (j + 1) * T)
        for s in range(4):
            g = j * 4 + s
            hbuf = make_h(K, xin, xin_name, g, xs, hb, gbc, "gbc", "b")
            bank = K.rot("pA", [0, 1, 2, 3])
            pst = K.ps[bank][:].bitcast(BF16)
            for c in range(8):
                P.pe(_tr(pst[:, c * 128:(c + 1) * 128], hb[hbuf][:, c * 128:(c + 1) * 128], K.ident_bf),
                     reads=[("hbb", hbuf), "idbf"], writes=[("ps", bank)])
            P.act(_actcopy(hT[:, :, s * 128:(s + 1) * 128], pst.rearrange("p (c t) -> p c t", c=8)),
                  reads=[("ps", bank)], writes=[HOB])
        hrd = [HOB]

        def proj(col0, M, wtok):
            bank = K.rot("pA", [0, 1, 2, 3])
            for k in range(8):
                mm(P, K.ps[bank][0:M, :], w_in[:, k, col0:col0 + M], hT[:, k, :], k == 0, k == 7,
                   hrd + [wtok + (k // 4,)], [("ps", bank)])
            return bank

        def conv_chunk(c):
            bank = K.rot("pA", [0, 1, 2, 3])
            for k in range(31):
                mm(P, K.ps[bank][:], DGp[:, c * 31 + k, :], up[:, c, k:k + T], k == 0, k == 30,
                   [("DGp", c), ("u", c)], [("ps", bank)])
            P.dve(_ts1(cv[:, c, :], K.ps[bank][:], mvec[:, MV_CB + c:MV_CB + c + 1], ALU.add),
                  reads=[("ps", bank), "mvec"], writes=[("cv", c)])

        for c in range(4):
            ba = proj(A0 + c * 128, 128, ("w_in", 0))
            bg = proj(G0 + c * 128, 128, ("w_in", 1))
            fi, tg = Fnew()
            P.act(_tanh_half(tg[:], K.ps[bg][:]), reads=[("ps", bg)], writes=[("F", fi)])
            P.dve(_stt(up[:, c, 30:30 + T], tg[:], 1.0, K.ps[ba][:], ALU.add, ALU.mult),
                  reads=[("F", fi), ("ps", ba)], writes=[("u", c)])
        sq_kv = []
        for c in range(3):
            bk = proj(CQ0 + c * 128, 128, ("w_in", 2))
            P.act(_act(cqg[:, c, :], K.ps[bk][:], AF.Copy, scale=mvec[:, MV_GQ + c:MV_GQ + c + 1]),
                  reads=[("ps", bk), "mvec"], writes=[("cqg", c)])
            fi, sq = Fnew()
            P.act(_act(sq[:], K.ps[bk][:], AF.Square), reads=[("ps", bk)], writes=[("F", fi)])
            mm(P, K.ps[6][:], K.ones32[:], sq[:], c == 0, c == 2, ["ones32", ("F", fi)], [("ps", 6)])
        for c in range(2):
            bk = proj(CKV0 + c * 128, 128, ("w_in", 3))
            P.act(_act(ckvg[:, c, :], K.ps[bk][:], AF.Copy, scale=mvec[:, MV_GKV + c:MV_GKV + c + 1]),
                  reads=[("ps", bk), "mvec"], writes=[("ckvg", c)])
            fi, sq = Fnew()
            P.act(_act(sq[:], K.ps[bk][:], AF.Square), reads=[("ps", bk)], writes=[("F", fi)])
            mm(P, K.ps[7][:], K.ones32[:], sq[:], c == 0, c == 1, ["ones32", ("F", fi)], [("ps", 7)])
            sq_kv.append((fi, sq))
        bkt = K.rot("pA", [0, 1, 2, 3])
        for s in range(4):
            for c in range(2):
                fi, sq = sq_kv[c]
                mm(P, K.ps[bkt][:, s:s + 1], sq[:, s * 128:(s + 1) * 128], K.ones32[:, 0:1], c == 0, c == 1,
                   ["ones32", ("F", fi)], [("ps", bkt)])
        bkr = proj(KR0, 96, ("w_in", 3))
        bkrs = proj(KRS0, 96, ("w_in", 3))
        f1, t1 = Fnew()
        f2, t2 = Fnew()
        P.dve(_tt(t1[R, :], K.ps[bkr][R, :], CS[R, 0, :], ALU.mult), reads=[("ps", bkr), "cs0"], writes=[("F", f1)])
        P.dve(_tt(t2[R, :], K.ps[bkrs][R, :], CS[R, 1, :], ALU.mult), reads=[("ps", bkrs), "cs1"], writes=[("F", f2)])
        P.dve(_tt(KTs[R, 0, :], t1[R, :], t2[R, :], ALU.add), reads=[("F", f1), ("F", f2)], writes=[("KTr", 0)])
        for h in range(1, 8):
            P.dve(_copy(KTs[R, h, :], KTs[R, 0, :]), reads=[("KTr", 0)], writes=[("KTr", h)])
        P.act(_act(RQ[:], K.ps[6][:], AF.Sqrt, bias=K.epsb[:, 0:1], scale=1.0 / 384), reads=[("ps", 6), "epsb"],
              writes=["RQ"])
        P.act(_act(RKV[:], K.ps[7][:], AF.Sqrt, bias=K.epsb[:, 0:1], scale=1.0 / 256), reads=[("ps", 7), "epsb"],
              writes=["RKV"])
        P.act(_act(RKT[:], K.ps[bkt][:, 0:4], AF.Sqrt, bias=K.epsb[:, 0:1], scale=1.0 / 256),
              reads=[("ps", bkt), "epsb"], writes=["RKT"])
        P.dve(_recip(RQ[:], RQ[:]), reads=["RQ"], writes=["RQ"])
        P.dve(_recip(RKV[:], RKV[:]), reads=["RKV"], writes=["RKV"])
        P.dve(_recip(RKT[:], RKT[:]), reads=["RKT"], writes=["RKT"])
        P.dve(_tt(CR[R, 0, :], CS[R, 0, :], RQ[R, :], ALU.mult), reads=["cs0", "RQ"], writes=["cr0"])
        P.dve(_tt(CR[R, 1, :], CS[R, 1, :], RQ[R, :], ALU.mult), reads=["cs1", "RQ"], writes=["cr1"])
        cq_tok = [("cqg", c) for c in range(3)]
        for h in range(8):
            bq = K.rot("pA", [0, 1, 2, 3])
            for c in range(3):
                mm(P, K.ps[bq][0:96, :], w_uq[:, c, h * 192:h * 192 + 96], cqg[:, c, :], c == 0, c == 2,
                   cq_tok + [("w_uq", c)], [("ps", bq)])
            bqs = K.rot("pA", [0, 1, 2, 3])
            for c in range(3):
                mm(P, K.ps[bqs][0:96, :], w_uq[:, c, h * 192 + 96:h * 192 + 192], cqg[:, c, :], c == 0, c == 2,
                   cq_tok + [("w_uq", c)], [("ps", bqs)])
            P.dve(_tt(QT[0:64, h, :], K.ps[bq][0:64, :], RQ[0:64, :], ALU.mult), reads=[("ps", bq), "RQ"],
                  writes=[("QTn", h)])
            f1, t1 = Fnew()
            f2, t2 = Fnew()
            P.dve(_tt(t1[R, :], K.ps[bq][R, :], CR[R, 0, :], ALU.mult), reads=[("ps", bq), "cr0"], writes=[("F", f1)])
            P.dve(_tt(t2[R, :], K.ps[bqs][R, :], CR[R, 1, :], ALU.mult), reads=[("ps", bqs), "cr1"],
                  writes=[("F", f2)])
            P.dve(_tt(QT[R, h, :], t1[R, :], t2[R, :], ALU.add), reads=[("F", f1), ("F", f2)], writes=[("QTr", h)])
        ckv_tok = [("ckvg", c) for c in range(2)]
        for h in range(8):
            bk = K.rot("pA", [0, 1, 2, 3])
            for c in range(2):
                mm(P, K.ps[bk][0:64, :], w_kn[:, c, h * 64:(h + 1) * 64], ckvg[:, c, :], c == 0, c == 1,
                   ckv_tok + ["w_kn"], [("ps", bk)])
            P.dve(_tt(KTs[0:64, h, :], K.ps[bk][0:64, :], RKV[0:64, :], ALU.mult), reads=[("ps", bk), "RKV"],
                  writes=[("KTn", h)])
        for s in range(4):
            bv = K.rot("pA", [0, 1, 2, 3])
            for c in range(2):
                mm(P, K.ps[bv][:], ckvg[:, c, s * 128:(s + 1) * 128], w_v[:, c, :], c == 0, c == 1,
                   ckv_tok + ["w_v"], [("ps", bv)])
            P.act(_act(Vs[:, :, s, 0:64], K.ps[bv][:].rearrange("p (h e) -> p h e", h=8), AF.Copy,
                       scale=RKT[:, s:s + 1]), reads=[("ps", bv), "RKT"], writes=[("Vs", s)])
        P.dma("pool", kt_d[:, j].rearrange("h p t -> p h t"), KTs[0:96, :, :],
              reads=[("KTn", h) for h in range(8)] + [("KTr", h) for h in range(8)], writes=[("ktd", j)])
        P.dma("pool", v_d[:, j].rearrange("h p e -> p h e"), Vs[:].rearrange("p h b e -> p h (b e)"),
              reads=[("Vs", s) for s in range(4)], writes=[("vd", j)])
        if j == 0:
            for c in range(4):
                for k in range(31):
                    P.dve(_ts1(DGp[:, c * 31 + k, :], K.ident_bf[:], wch[:, c, k:k + 1], ALU.mult),
                          reads=["idbf", "wch"], writes=[("DGp", c)])
        vs_tok = [("Vs", s) for s in range(4)]
        chunks = [(h_, jp_) for h_ in range(8) for jp_ in range(j)]
        issued = [0]

        def ensure(n):
            while issued[0] < min(n, len(chunks)):
                h_, jp_ = chunks[issued[0]]
                cb_ = (kv_base + issued[0]) % 4
                P.dma("sp", KTc[cb_][0:96, :], kt_d[h_, jp_], reads=[("ktd", jp_)], writes=[("KTc", cb_)])
                P.dma("sp", Vc[cb_][:].rearrange("p b e -> p (b e)"), v_d[h_, jp_], reads=[("vd", jp_)],
                      writes=[("Vc", cb_)])
                issued[0] += 1

        ln_state = {}

        def ln_a():
            ln_state["sq"] = []
            for c in range(4):
                fi, sq = Fnew()
                P.dve(_tt(sq[:], cv[:, c, :], cv[:, c, :], ALU.mult), reads=[("cv", c)], writes=[("F", fi)])
                ln_state["sq"].append((fi, sq))
            for c in range(4):
                mm(P, K.ps[7][:], K.ones32[:], cv[:, c, :], c == 0, c == 3, ["ones32", ("cv", c)], [("ps", 7)])

        def ln_b():
            P.dve(_ts1(MEAN[:], K.ps[7][:], 1.0 / 512, ALU.mult), reads=[("ps", 7)], writes=["MEAN"])
            for c in range(4):
                fi, sq = ln_state["sq"][c]
                mm(P, K.ps[7][:], K.ones32[:], sq[:], c == 0, c == 3, ["ones32", ("F", fi)], [("ps", 7)])

        def ln_c1():
            fi, msq = Fnew()
            P.dve(_tt(msq[:], MEAN[:], MEAN[:], ALU.mult), reads=["MEAN"], writes=[("F", fi)])
            P.dve(_stt(RLN[:], K.ps[7][:], 1.0 / 512, msq[:], ALU.mult, ALU.subtract), reads=[("ps", 7), ("F", fi)],
                  writes=["RLN"])
            P.act(_act(RLN[:], RLN[:], AF.Sqrt, bias=K.epsb[:, 0:1]), reads=["RLN", "epsb"], writes=["RLN"])
            P.dve(_recip(RLN[:], RLN[:]), reads=["RLN"], writes=["RLN"])
            for c in range(4):
                P.dve(_tt(cv[:, c, :], cv[:, c, :], MEAN[:], ALU.subtract), reads=[("cv", c), "MEAN"],
                      writes=[("cv", c)])
                P.dve(_tt(cv[:, c, :], cv[:, c, :], RLN[:], ALU.mult), reads=[("cv", c), "RLN"], writes=[("cv", c)])

        def ln_c2():
            tts = []
            for c in range(4):
                ft, tt_ = Fnew()
                P.act(_act(tt_[:], cv[:, c, :], AF.Tanh, bias=lnh[:, 4 + c:5 + c], scale=lnh[:, c:c + 1]),
                      reads=[("cv", c), "lnh"], writes=[("F", ft)])
                tts.append((ft, tt_))
            for c in range(4):
                ft, tt_ = tts[c]
                P.dve(_ts(cv[:, c, :], cv[:, c, :], lnh[:, c:c + 1], lnh[:, 4 + c:5 + c], ALU.mult, ALU.add),
                      reads=[("cv", c), "lnh"], writes=[("cv", c)])
                P.dve(_stt(UT[:, c, :], tt_[:], 1.0, cv[:, c, :], ALU.add, ALU.mult), reads=[("F", ft), ("cv", c)],
                      writes=[("UT", c)])

        pending_fin = [None]
        for h in range(8):
            bo = 4 + (h % 2)
            blocks = []
            for jp in range(j):
                ci = h * j + jp
                cb = (kv_base + ci) % 4
                for b4 in range(4):
                    blocks.append((KTc[cb][0:96, b4 * 128:(b4 + 1) * 128], Vc[cb][:, b4, :], 0, False,
                                   [("KTc", cb)], [("Vc", cb)], ci if b4 == 0 else None))
            for b in range(4):
                blocks.append((KTs[0:96, h, b * 128:(b + 1) * 128], Vs[:, h, b, :], b * 128, True,
                               [("KTn", h), ("KTr", h)], vs_tok, None))
            nb = len(blocks)
            LOOK = 2
            pts = {}

            def emit_S(bi, h=h, blocks=blocks, pts=pts):
                kT, vv, q0, diag, ktok, vtok, ci0 = blocks[bi]
                if ci0 is not None:
                    ensure(ci0 + 3)
                bs = K.rot("pA", [0, 1, 2, 3])
                mm(P, K.ps[bs][:, q0:T], kT, QT[0:96, h, q0:T], True, True,
                   ktok + [("QTn", h), ("QTr", h)], [("ps", bs)])
                pi = K.rot("PT", [0, 1, 2, 3])
                pts[bi] = pi
                P.act(_act(PT[pi][:, q0:T], K.ps[bs][:, q0:T], AF.Exp, scale=ATT_SCALE), reads=[("ps", bs)],
                      writes=[("PT", pi)])
                if diag:
                    P.dve(_tt(PT[pi][:, q0:q0 + 128], PT[pi][:, q0:q0 + 128], mask[:], ALU.mult),
                          reads=[("PT", pi), "mask"], writes=[("PT", pi)])

            def emit_PV(bi, h=h, blocks=blocks, pts=pts, bo=bo, nb=nb):
                kT, vv, q0, diag, ktok, vtok, ci0 = blocks[bi]
                pi = pts[bi]
                mm(P, K.ps[bo][0:65, q0:T], vv, PT[pi][:, q0:T], bi == 0, bi == nb - 1,
                   vtok + [("PT", pi)], [("ps", bo)])

            for bi in range(nb + LOOK):
                if bi < nb:
                    emit_S(bi)
                if bi == LOOK and pending_fin[0] is not None:
                    pending_fin[0]()
                    pending_fin[0] = None
                if bi >= LOOK:
                    emit_PV(bi - LOOK)
            P.dve(_recip(RD[64:65, :], K.ps[bo][64:65, :]), reads=[("ps", bo)], writes=["RD"])

            def fin(h=h, bo=bo):
                bb = 6
                mm(P, K.ps[bb][0:64, :], K.ones32[64:65, 0:64], RD[64:65, :], True, True, ["ones32", "RD"],
                   [("ps", bb)])
                fi, bcs = Fnew()
                P.dve(_copy(bcs[0:64, :], K.ps[bb][0:64, :]), reads=[("ps", bb)], writes=[("F", fi)])
                hp = (h % 2) * 64
                P.dve(_tt(oT[hp:hp + 64, h // 2, :], K.ps[bo][0:64, :], bcs[0:64, :], ALU.mult),
                      reads=[("ps", bo), ("F", fi)], writes=[HOB])

            pending_fin[0] = fin
            if h < 4:
                conv_chunk(h)
            if j + 1 < NT:
                if h == 1:
                    rope_dve(j + 1)
                elif h == 2:
                    rope_act()
            if h == 3:
                ln_a()
            elif h == 4:
                ln_b()
            elif h == 5:
                ln_c1()
            elif h == 6:
                ln_c2()
        pending_fin[0]()
        P.pool(_copy(up[:, :, 0:30], up[:, :, T:T + 30]), reads=[("u", c) for c in range(4)],
               writes=[("u", c) for c in range(4)])
        kv_base += len(chunks)
        ut_tok = [("UT", c) for c in range(4)]
        for s in range(4):
            g = j * 4 + s
            b = K.rot("xs", [0, 1, 2])
            P.dma("sp", xs[b][:], xrow(xin, g), reads=[("xd", xin_name, g)], writes=[("xs", b)])
            for half in range(2):
                by = K.rot("pY", [0, 1, 2, 3])
                hs = slice(half * 512, (half + 1) * 512)
                for c in range(4):
                    mm(P, K.ps[by][:], UT[:, c, s * 128:(s + 1) * 128], w_oc[:, c, hs], c == 0, False,
                       ut_tok + ["w_oc%d" % c], [("ps", by)])
                for h2 in range(4):
                    mm(P, K.ps[by][:], oT[:, h2, s * 128:(s + 1) * 128], w_oh[:, h2, hs], False, h2 == 3,
                       [HOB, "w_oh%d" % h2], [("ps", by)])
                P.dve(_tt(xs[b][:, hs], xs[b][:, hs], K.ps[by][:], ALU.add), reads=[("ps", by), ("xs", b)],
                      writes=[("xs", b)])
            P.act(_sq_accum(junk, xs[b], K.ss_all, g), reads=[("xs", b)], writes=[("ss", g), "junk"])
            P.dma("pool", xrow(xout, g), xs[b][:], reads=[("xs", b)], writes=[("xd", xout_name, g)])


def pack_weights(inp):
    f = lambda a: np.ascontiguousarray(np.asarray(a, dtype=np.float32))
    W = {}
    W["w_up"] = f(inp["w_up"])
    W["w_down"] = f(inp["w_down"])
    fw = np.asarray(inp["ffn_conv_w"], np.float32)
    fb = np.asarray(inp["ffn_conv_b"], np.float32)
    fv = np.concatenate([fw, fb[:, None, :]], axis=1)
    fv = fv.reshape(2, 4, NF, 128).transpose(0, 3, 2, 1)
    W["ffn_vec"] = f(fv.reshape(2, 128, NF * 4))
    W["norm_ffn"] = f(np.asarray(inp["norm_ffn"]).reshape(2, 1, D))
    W["final_norm"] = f(np.asarray(inp["final_norm"]).reshape(1, D))
    W["norm_mix_o"] = f(np.asarray(inp["norm_mix_o"]).reshape(1, D))
    W["pool_scale"] = f(np.asarray(inp["pool_scale"]).reshape(1, D))
    W["pool_w"] = f(np.asarray(inp["pool_w"])[0])
    w_in = np.asarray(inp["w_in"], np.float32)[0]
    kr = w_in[:, 1664:1696]
    krs = np.concatenate([kr[:, 16:32], kr[:, 0:16]], axis=1)
    W["w_in_ext"] = f(np.concatenate([w_in[:, 0:1664], kr, kr, kr, krs, krs, krs], axis=1))
    w_uq = np.asarray(inp["w_uq"], np.float32)[0].reshape(384, 8, 96)
    qs = np.concatenate([w_uq[:, :, 0:64], w_uq[:, :, 80:96], w_uq[:, :, 64:80]], axis=2)
    W["w_uq_ext"] = f(np.concatenate([w_uq, qs], axis=2).reshape(384, 8 * 192))
    w_ukv = np.asarray(inp["w_ukv"], np.float32)[0].reshape(256, 8, 128)
    W["w_kn"] = f(w_ukv[:, :, 0:64].reshape(256, 512))
    W["w_v"] = f(w_ukv[:, :, 64:128].reshape(256, 512))
    W["w_out"] = f(np.asarray(inp["w_out"])[0])
    W["norm_mix_e"] = f(np.asarray(inp["norm_mix_e"]).reshape(1, D))
    mv = np.zeros((128, MV_COLS), np.float32)
    cw = np.asarray(inp["conv_w"], np.float32)[0]
    mv[:, MV_CW:MV_CW + 124] = cw.T.reshape(4, 128, 31).transpose(1, 0, 2).reshape(128, 124)
    pc = lambda v, n: np.asarray(v, np.float32).reshape(n, 128).T
    mv[:, MV_CB:MV_CB + 4] = pc(inp["conv_b"][0], 4)
    mv[:, MV_LG:MV_LG + 4] = pc(inp["conv_ln_g"][0], 4)
    mv[:, MV_LB:MV_LB + 4] = pc(inp["conv_ln_b"][0], 4)
    mv[:, MV_GQ:MV_GQ + 3] = pc(inp["q_norm_g"][0], 3)
    mv[:, MV_GKV:MV_GKV + 2] = pc(inp["kv_norm_g"][0], 2)
    invf = (np.float32(1.0) / (np.float32(10000.0) ** (np.arange(0, 32, 2, dtype=np.float32) / np.float32(32))))
    mv[64:96, MV_INVF] = np.concatenate([invf, invf]).astype(np.float32)
    mv[64:80, MV_SGN] = -1.0
    mv[80:96, MV_SGN] = 1.0
    W["mix_vec"] = f(mv)
    return W


def run_phases(phases, xs_list, W, ss_list=None, n_cores=8, pos=None):
    need_stats = ss_list is None
    nc, K = build(phases, need_stats)
    in_maps = []
    for c in range(n_cores):
        m = {}
        for name in K.din:
            if name == "xin":
                m[name] = xs_list[c]
            elif name == "ss_in":
                m[name] = ss_list[c]
            elif name == "pos":
                m[name] = pos[c]
            else:
                w = W[name]
                if K.din[name][1]:
                    padrow = np.full(w.shape[:-2] + (1, w.shape[-1]), float(c), w.dtype)
                    w = np.concatenate([w, padrow], axis=-2)
                m[name] = w
        in_maps.append(m)
    res = run_bass_kernel_spmd(nc, in_maps, core_ids=list(range(n_cores)))
    return [r["xout"] for r in res.results], [r["ss_out"] for r in res.results]


FUSED = True


def kernel(**inputs):
    n = 8
    W = pack_weights(inputs)
    x = np.asarray(inputs["x"], np.float32)
    pos = np.asarray(inputs["positions"], np.int32)
    xs_list = [np.ascontiguousarray(x[c]) for c in range(n)]
    pos_list = [np.ascontiguousarray(pos[c:c + 1]) for c in range(n)]
    if FUSED:
        outs, _ = run_phases(list(PHASES_ALL), xs_list, W, None, n_cores=n, pos=pos_list)
    else:
        outs, ss = run_phases(["mix0"], xs_list, W, None, n_cores=n, pos=pos_list)
        for ph in ("ffn0", "mix1", "ffn1", "final"):
            outs, ss = run_phases([ph], outs, W, ss, n_cores=n, pos=pos_list)
    return np.stack(outs, axis=0).astype(np.float32)
```
